# Optimizing a Trainium2 kernel written in Bass

```python
import jax
import jax.numpy as jnp
from jax import lax
import numpy as np

D_MODEL = 2048
BATCH = 8
SEQ = 2048
DEPTH = 2
DEC_BATCH = 32
DEC_SEQ = 64
PAST_LEN = 1024

CHUNK = 64
N_MIXERS = 2
A_DK = 128
A_HEADS = D_MODEL // A_DK
A_DV = D_MODEL // A_HEADS
A_FD = A_HEADS * A_DK
B_HEAD_DIM = 64
B_Q_HEADS = D_MODEL // B_HEAD_DIM
B_KV_HEADS = B_Q_HEADS // 8
B_GROUP = B_Q_HEADS // B_KV_HEADS
WINDOW = 128
W_CHUNKS = -(-WINDOW // CHUNK)
FF_DIM = -(-8 * D_MODEL // (3 * 256)) * 256
PLE_DIM = 256
RMS_EPS = 1e-6

kernel_name = 'hybrid_hgrn2_swa_sink_stream_step'


def _rms(x, g):
    xf = x.astype(jnp.float32)
    y = xf * lax.rsqrt(jnp.mean(xf * xf, axis=-1, keepdims=True) + RMS_EPS)
    return (y * g.astype(jnp.float32)).astype(x.dtype)


def _gla_scan(q, k, v, logf, s0):
    b, L, h, _ = q.shape
    dv = v.shape[-1]
    c = min(CHUNK, L)
    n = L // c

    def blocks(t):
        return t.reshape(b, n, c, h, t.shape[-1]).transpose(1, 0, 3, 2, 4)

    causal = jnp.tril(jnp.ones((c, c), dtype=bool))

    def step(S, inp):
        qc, kc, vc, gc = inp
        G = jnp.cumsum(gc, axis=2)
        diff = G[:, :, :, None, :] - G[:, :, None, :, :]
        decay = jnp.exp(jnp.where(causal[:, :, None], diff, -jnp.inf))
        A = jnp.einsum('bhtd,bhtsd,bhsd->bhts', qc, decay, kc)
        o = A @ vc + jnp.einsum('bhtd,bhde->bhte', qc * jnp.exp(G), S)
        G_last = G[:, :, -1:, :]
        S = jnp.exp(G_last[:, :, 0, :, None]) * S + jnp.einsum('bhsd,bhse->bhde', kc * jnp.exp(G_last - G), vc)
        return S, o

    S, o = lax.scan(step, s0.astype(jnp.float32), (blocks(q), blocks(k), blocks(v), blocks(logf)))
    o = o.transpose(1, 0, 3, 2, 4).reshape(b, L, h, dv)
    return o, S


def _hgrn2(xn, w_in, lb, g_norm, w_o, s0):
    b, L, _ = xn.shape
    proj = xn @ w_in
    q, f, i, g = jnp.split(proj, [A_FD, 2 * A_FD, 2 * A_FD + A_HEADS * A_DV], axis=-1)
    q = jax.nn.silu(q.astype(jnp.float32))
    forget = lb + (1.0 - lb) * jax.nn.sigmoid(f.astype(jnp.float32))
    k = 1.0 - forget
    logf = jnp.log(forget)

    def heads(t):
        return t.reshape(b, L, A_HEADS, -1)

    o, S = _gla_scan(heads(q), heads(k), heads(i.astype(jnp.float32)), heads(logf), s0)
    o = _rms(o.reshape(b, L, A_HEADS * A_DV), g_norm) * jax.nn.silu(g.astype(jnp.float32))
    return o.astype(xn.dtype) @ w_o, S


def _alibi_slopes():
    return 2.0 ** (-8.0 * jnp.arange(1, B_Q_HEADS + 1, dtype=jnp.float32) / B_Q_HEADS)


def _qkv(xn, w_qkv, q_gain, k_gain):
    b, L, _ = xn.shape
    qkv = xn @ w_qkv
    q, k, v = jnp.split(qkv, [B_Q_HEADS * B_HEAD_DIM, (B_Q_HEADS + B_KV_HEADS) * B_HEAD_DIM], axis=-1)
    q = _rms(q.reshape(b, L, B_KV_HEADS, B_GROUP, B_HEAD_DIM), q_gain)
    k = _rms(k.reshape(b, L, B_KV_HEADS, B_HEAD_DIM), k_gain)
    v = v.reshape(b, L, B_KV_HEADS, B_HEAD_DIM)
    return q, k, v


def _sink_attend(q, k, v, bias, sinks):
    s = jnp.einsum('bnqkgd,bnskd->bnkgqs', q, k).astype(jnp.float32) * (B_HEAD_DIM ** -0.5) + bias
    sink = sinks.astype(jnp.float32)[:, :, None]
    m = jnp.maximum(s.max(-1), sink)
    p = jnp.exp(s - m[..., None])
    den = p.sum(-1) + jnp.exp(sink - m)
    w = (p / den[..., None]).astype(v.dtype)
    return jnp.einsum('bnkgqs,bnskd->bnqkgd', w, v)


def _swa_prompt(xn, w_qkv, q_gain, k_gain, sinks, w_o):
    b, L, _ = xn.shape
    n = L // CHUNK
    q, k, v = _qkv(xn, w_qkv, q_gain, k_gain)
    qb = q.reshape(b, n, CHUNK, B_KV_HEADS, B_GROUP, B_HEAD_DIM)

    def band(t):
        tp = jnp.pad(t, ((0, 0), (W_CHUNKS * CHUNK, 0), (0, 0), (0, 0)))
        tp = tp.reshape(b, n + W_CHUNKS, CHUNK, B_KV_HEADS, B_HEAD_DIM)
        return jnp.concatenate([tp[:, w:w + n] for w in range(W_CHUNKS + 1)], axis=2)

    kb, vb = band(k), band(v)
    lk = (W_CHUNKS + 1) * CHUNK
    qi = jnp.arange(CHUNK)
    kj = jnp.arange(lk)
    dist = jnp.abs(W_CHUNKS * CHUNK + qi[:, None] - kj[None, :]).astype(jnp.float32)
    alibi = (-_alibi_slopes()[:, None, None] * dist).reshape(B_KV_HEADS, B_GROUP, CHUNK, lk)
    kpos = (jnp.arange(n)[:, None] - W_CHUNKS) * CHUNK + kj[None, :]
    bias = jnp.where((kpos >= 0)[:, None, None, None, :], alibi, -jnp.inf)
    o = _sink_attend(qb, kb, vb, bias, sinks.reshape(B_KV_HEADS, B_GROUP))
    y = o.reshape(b, L, B_Q_HEADS * B_HEAD_DIM) @ w_o
    keep = min(WINDOW, L)
    return y, k[:, L - keep:], v[:, L - keep:]


def _swa_sample(xn, ck, cv, w_qkv, q_gain, k_gain, sinks, w_o):
    b, L, _ = xn.shape
    q, k, v = _qkv(xn, w_qkv, q_gain, k_gain)
    nc = ck.shape[1]
    k_all = jnp.concatenate([ck.astype(k.dtype), k], axis=1)[:, None]
    v_all = jnp.concatenate([cv.astype(v.dtype), v], axis=1)[:, None]
    qpos = PAST_LEN + jnp.arange(L)
    kpos = jnp.concatenate([PAST_LEN - nc + jnp.arange(nc), qpos])
    gap = qpos[:, None] // CHUNK - kpos[None, :] // CHUNK
    visible = (gap >= 0) & (gap <= W_CHUNKS)
    dist = jnp.abs(qpos[:, None] - kpos[None, :]).astype(jnp.float32)
    alibi = (-_alibi_slopes()[:, None, None] * dist).reshape(B_KV_HEADS, B_GROUP, L, nc + L)
    bias = jnp.where(visible, alibi, -jnp.inf)
    o = _sink_attend(q[:, None], k_all, v_all, bias, sinks.reshape(B_KV_HEADS, B_GROUP))
    y = o.reshape(b, L, B_Q_HEADS * B_HEAD_DIM) @ w_o
    return y, k, v


def _ffn_ple(h, p_i, n_ffn, w_gu, w_down, n_ple, w_ple_proj, w_ple_gate):
    gate, up = jnp.split(_rms(h, n_ffn) @ w_gu, 2, axis=-1)
    h = h + (jax.nn.silu(gate) * up) @ w_down
    g = jax.nn.sigmoid(_rms(h, n_ple) @ w_ple_gate)
    return h + (p_i.astype(h.dtype) @ w_ple_proj) * g


def setup_inputs(seed: int = 0) -> dict:
    key = jax.random.key(seed)
    k = jax.random.split(key, 24)
    f32 = jnp.float32

    def nrm(kk, shape, scale=1.0):
        return jax.random.normal(kk, shape, f32) * scale

    def gain(kk, shape):
        return 1.0 + 0.05 * jax.random.normal(kk, shape, f32)

    n_a = (DEPTH + 1) // 2
    n_b = DEPTH // 2
    n_cache = min(WINDOW, PAST_LEN)
    return {
        'x_prompt': nrm(k[0], (BATCH, SEQ, D_MODEL)),
        'x_sample': nrm(k[1], (DEC_BATCH, DEC_SEQ, D_MODEL)),
        'state_hgrn': nrm(k[2], (n_a, DEC_BATCH, A_HEADS, A_DK, A_DV), 0.5),
        'cache_k': nrm(k[3], (n_b, DEC_BATCH, n_cache, B_KV_HEADS, B_HEAD_DIM)),
        'cache_v': nrm(k[4], (n_b, DEC_BATCH, n_cache, B_KV_HEADS, B_HEAD_DIM)),
        'p_prompt': nrm(k[5], (DEPTH, BATCH, SEQ, PLE_DIM)),
        'p_sample': nrm(k[6], (DEPTH, DEC_BATCH, DEC_SEQ, PLE_DIM)),
        'norm_mix': gain(k[7], (DEPTH, D_MODEL)),
        'norm_ffn': gain(k[8], (DEPTH, D_MODEL)),
        'norm_ple': gain(k[9], (DEPTH, D_MODEL)),
        'a_w_in': nrm(k[10], (n_a, D_MODEL, 2 * A_FD + 2 * A_HEADS * A_DV), D_MODEL ** -0.5),
        'a_lb_logits': nrm(k[11], (DEPTH + 1, A_FD), 0.5),
        'a_g_norm': gain(k[12], (n_a, A_HEADS * A_DV)),
        'a_w_o': nrm(k[13], (n_a, A_HEADS * A_DV, D_MODEL), (A_HEADS * A_DV) ** -0.5),
        'b_w_qkv': nrm(k[14], (n_b, D_MODEL, (B_Q_HEADS + 2 * B_KV_HEADS) * B_HEAD_DIM), D_MODEL ** -0.5),
        'b_q_norm': gain(k[15], (n_b, B_HEAD_DIM)),
        'b_k_norm': gain(k[16], (n_b, B_HEAD_DIM)),
        'b_sinks': nrm(k[17], (n_b, B_Q_HEADS), 0.5),
        'b_w_o': nrm(k[18], (n_b, B_Q_HEADS * B_HEAD_DIM, D_MODEL), (B_Q_HEADS * B_HEAD_DIM) ** -0.5),
        'f_w_gu': nrm(k[19], (DEPTH, D_MODEL, 2 * FF_DIM), D_MODEL ** -0.5),
        'f_w_down': nrm(k[20], (DEPTH, FF_DIM, D_MODEL), FF_DIM ** -0.5),
        'ple_w_proj': nrm(k[21], (DEPTH, PLE_DIM, D_MODEL), PLE_DIM ** -0.5),
        'ple_w_gate': nrm(k[22], (DEPTH, D_MODEL, D_MODEL), D_MODEL ** -0.5),
    }


def reference(x_prompt, x_sample, state_hgrn, cache_k, cache_v, p_prompt, p_sample,
              norm_mix, norm_ffn, norm_ple, a_w_in, a_lb_logits, a_g_norm, a_w_o,
              b_w_qkv, b_q_norm, b_k_norm, b_sinks, b_w_o, f_w_gu, f_w_down,
              ple_w_proj, ple_w_gate):
    lower_bounds = jnp.cumsum(jax.nn.softmax(a_lb_logits.astype(jnp.float32), axis=0), axis=0)
    hp, hs = x_prompt, x_sample
    st_p, st_s, kp_l, vp_l, ks_l, vs_l = [], [], [], [], [], []
    for i in range(DEPTH):
        j = i // N_MIXERS
        xp = _rms(hp, norm_mix[i])
        xs = _rms(hs, norm_mix[i])
        if i % N_MIXERS == 0:
            s0 = jnp.zeros((hp.shape[0], A_HEADS, A_DK, A_DV), jnp.float32)
            mp, sp = _hgrn2(xp, a_w_in[j], lower_bounds[i], a_g_norm[j], a_w_o[j], s0)
            ms, ss = _hgrn2(xs, a_w_in[j], lower_bounds[i], a_g_norm[j], a_w_o[j], state_hgrn[j])
            st_p.append(sp)
            st_s.append(ss)
        else:
            mp, kp, vp = _swa_prompt(xp, b_w_qkv[j], b_q_norm[j], b_k_norm[j], b_sinks[j], b_w_o[j])
            ms, kn, vn = _swa_sample(xs, cache_k[j], cache_v[j], b_w_qkv[j], b_q_norm[j], b_k_norm[j],
                                     b_sinks[j], b_w_o[j])
            kp_l.append(kp)
            vp_l.append(vp)
            ks_l.append(kn)
            vs_l.append(vn)
        hp = hp + mp
        hs = hs + ms
        hp = _ffn_ple(hp, p_prompt[i], norm_ffn[i], f_w_gu[i], f_w_down[i], norm_ple[i], ple_w_proj[i], ple_w_gate[i])
        hs = _ffn_ple(hs, p_sample[i], norm_ffn[i], f_w_gu[i], f_w_down[i], norm_ple[i], ple_w_proj[i], ple_w_gate[i])
    state_hgrn_prompt = jnp.stack(st_p)
    state_hgrn_sample = jnp.stack(st_s)
    cache_k_prompt = jnp.stack(kp_l)
    cache_v_prompt = jnp.stack(vp_l)
    cache_k_sample = jnp.stack(ks_l)
    cache_v_sample = jnp.stack(vs_l)
    return (hp, hs, state_hgrn_prompt, state_hgrn_sample, cache_k_prompt, cache_v_prompt, cache_k_sample, cache_v_sample)
```

```python
import contextlib
import numpy as np
import concourse.bass as bass
import concourse.mybir as mybir
from concourse.bass_utils import run_bass_kernel_spmd

F32 = mybir.dt.float32
BF16 = mybir.dt.bfloat16
AF = mybir.ActivationFunctionType
ALU = mybir.AluOpType
AX = mybir.AxisListType

PE, ACT, DVE, POOL, SP = "pe", "act", "dve", "pool", "sp"
ENGINES = (PE, ACT, DVE, POOL, SP)

D = 2048
KC = 16
FF = 5632
FC = 44
SEQ = 2048
NSEQ_S = 4
LS = 64
EPS = 1e-6
TMAX = 512
NSLOT = 3
DMASK = 1.0e5


class Op:
    __slots__ = ("eng", "seq", "fn", "waits", "needs_inc", "is_dma", "sem_key", "dma_idx", "tick")

    def __init__(self, eng, seq, fn, is_dma=False, sem_key=None):
        self.eng = eng
        self.seq = seq
        self.fn = fn
        self.waits = []
        self.needs_inc = False
        self.is_dma = is_dma
        self.sem_key = sem_key
        self.dma_idx = 0
        self.tick = 0


class Prog:
    def __init__(self, nc):
        self.nc = nc
        self.q = {e: [] for e in ENGINES}
        self.last_w = {}
        self.readers = {}
        self.seen = {c: {p: -1 for p in ENGINES} for c in ENGINES}
        self.dma_seen = {c: set() for c in ENGINES}
        self.dma_count = {}
        self.dma_last = {}
        self.out_dmas = []

    def _add_dep(self, op, dep):
        if dep is None or dep is op:
            return
        if dep.is_dma:
            if id(dep) in self.dma_seen[op.eng]:
                return
            self.dma_seen[op.eng].add(id(dep))
            op.waits.append(dep)
            return
        if dep.eng == PE and op.eng == PE:
            return
        if dep.seq <= self.seen[op.eng][dep.eng]:
            return
        self.seen[op.eng][dep.eng] = dep.seq
        dep.needs_inc = True
        op.waits.append(dep)

    def _track(self, op, reads, writes):
        best = {}
        dmas = []

        def cand(d):
            if d is None or d is op:
                return
            if d.is_dma:
                dmas.append(d)
            elif d.eng not in best or d.seq > best[d.eng].seq:
                best[d.eng] = d

        for k in reads:
            cand(self.last_w.get(k))
        for k in writes:
            cand(self.last_w.get(k))
            for r in self.readers.get(k, ()):
                cand(r)
        for d in dmas:
            self._add_dep(op, d)
        for d in best.values():
            self._add_dep(op, d)
        for k in reads:
            self.readers.setdefault(k, []).append(op)
        for k in writes:
            self.last_w[k] = op
            self.readers[k] = []

    def op(self, eng, fn, reads=(), writes=()):
        o = Op(eng, len(self.q[eng]), fn)
        self._track(o, reads, writes)
        self.q[eng].append(o)
        return o

    def dma(self, eng, fn, reads=(), writes=(), sem_key=None, is_output=False):
        o = Op(eng, len(self.q[eng]), fn, is_dma=True, sem_key=sem_key)
        prev = self.dma_last.get(sem_key)
        if prev is not None:
            self._add_dep(o, prev)
        self._track(o, reads, writes)
        o.dma_idx = self.dma_count.get(sem_key, 0)
        self.dma_count[sem_key] = o.dma_idx + 1
        self.dma_last[sem_key] = o
        self.q[eng].append(o)
        if is_output:
            self.out_dmas.append(o)
        return o

    def emit(self):
        nc = self.nc
        fin = Op(SP, len(self.q[SP]), None)
        lastout = {}
        for o in self.out_dmas:
            lastout[o.sem_key] = o
        for o in lastout.values():
            fin.waits.append(o)
        self.q[SP].append(fin)
        EPOCH = 1500
        nep = {}
        for e in ENGINES:
            t = 0
            for o in self.q[e]:
                if o.needs_inc:
                    t += 1
                o.tick = t
            nep[e] = t // EPOCH + 1
        sem_names = [(e, k) for e in ENGINES for k in range(nep[e])] + [("dma", k) for k in self.dma_count]
        with contextlib.ExitStack() as st:
            sems = {}
            for i, n in enumerate(sem_names):
                sems[n] = st.enter_context(nc.semaphore(f"s{i}"))
            block = st.enter_context(nc.Block())

            def run(engname, eng):
                for o in self.q[engname]:
                    for d in o.waits:
                        if d.is_dma:
                            eng.wait_ge(sems[("dma", d.sem_key)], 16 * (d.dma_idx + 1))
                        else:
                            eng.wait_ge(sems[(d.eng, (d.tick - 1) // EPOCH)], (d.tick - 1) % EPOCH + 1)
                    if o.fn is None:
                        continue
                    ins = o.fn(eng)
                    if o.is_dma:
                        ins.then_inc(sems[("dma", o.sem_key)], 16)
                    elif o.needs_inc:
                        ins.then_inc(sems[(engname, (o.tick - 1) // EPOCH)], 1)

            @block.tensor
            def _(eng):
                run(PE, eng)

            @block.scalar
            def _(eng):
                run(ACT, eng)

            @block.vector
            def _(eng):
                run(DVE, eng)

            @block.gpsimd
            def _(eng):
                run(POOL, eng)

            @block.sync
            def _(eng):
                run(SP, eng)


def alibi_slope(h):
    return float(2.0 ** (-8.0 * (h + 1) / 32.0))


def build(tiles, dbg=None):
    nc = bass.Bass("TRN2", target_bir_lowering=False)

    def din(name, shape):
        return nc.dram_tensor(name, list(shape), F32, kind="ExternalInput").ap()

    def dout(name, shape):
        return nc.dram_tensor(name, list(shape), F32, kind="ExternalOutput").ap()

    xp = din("xp", [SEQ, D])
    xs = din("xs", [NSEQ_S * LS, D])
    st_in = din("st", [NSEQ_S, 16, 128, 128])
    ck_in = din("ck", [NSEQ_S, 128, 256])
    cv_in = din("cv", [NSEQ_S, 128, 256])
    pp = din("pp", [2, SEQ, 256])
    psm = din("psm", [2, NSEQ_S * LS, 256])
    nmix = din("nmix", [2, D])
    nffn = din("nffn", [2, D])
    nple = din("nple", [2, D])
    w_in = din("w_in", [D, 8192])
    lbl = din("lbl", [3, D])
    gnorm = din("gnorm", [1, D])
    a_wo = din("a_wo", [D, D])
    wqkv = din("wqkv", [D, 2560])
    qnorm = din("qnorm", [1, 64])
    knorm = din("knorm", [1, 64])
    sinks = din("sinks", [1, 32])
    b_wo = din("b_wo", [D, D])
    wgu = din("wgu", [2, D, 2 * FF])
    wdown = din("wdown", [2, FF, D])
    wpp = din("wpp", [2, 256, D])
    wpg = din("wpg", [2, D, D])

    yp = dout("yp", [SEQ, D])
    ys = dout("ys", [NSEQ_S * LS, D])
    sp_out = dout("sp", [16, 128, 128])
    ss_out = dout("ss", [NSEQ_S, 16, 128, 128])
    ckp = dout("ckp", [128, 256])
    cvp = dout("cvp", [128, 256])
    cks = dout("cks", [NSEQ_S * LS, 256])
    cvs = dout("cvs", [NSEQ_S * LS, 256])

    NSTR = 121
    wscr = nc.dram_tensor("wscr", [NSTR, 128, 8192], BF16, kind="Internal").ap()

    P = Prog(nc)
    st = contextlib.ExitStack()
    with st:
        def sb(name, shape, dt):
            return st.enter_context(nc.sbuf_tensor(name, shape, dt))

        h = sb("h", [128, KC, TMAX], F32)
        xn = sb("xn", [128, KC, TMAX], BF16)
        R = sb("R", [128, 22528], BF16)
        Rf = R.bitcast(F32)
        Q = sb("Q", [128, 12288], BF16)
        Qf = Q.bitcast(F32)
        W = [sb(f"W{i}", [128, 8192], BF16) for i in range(NSLOT)]
        sstate = sb("sstate", [128, 16, 128], F32)
        sbf = sb("sbf", [128, 2, 8, 128], BF16)
        tmp = [sb(f"tmp{i}", [128, TMAX], F32) for i in range(8)]
        sx = sb("sx", [128, 2, 128], F32)
        sq = [sb(f"sq{i}", [128, TMAX], BF16) for i in range(2)]
        rstd = sb("rstd", [128, TMAX], F32)
        pT = sb("pT", [128, 2, TMAX], BF16)
        egl = sb("egl", [128, 8, 8], F32)
        ident = sb("ident", [128, 128], F32)
        ones_bf = sb("ones_bf", [128, 128], BF16)
        bd_bf = sb("bd_bf", [128, 128], BF16)
        scanmask = sb("scanmask", [128, TMAX], BF16)
        bdmask = sb("bdmask", [128, 128], F32)
        D0 = sb("D0", [128, 128], F32)
        D1 = sb("D1", [128, 128], F32)
        D1s = sb("D1s", [128, 128], F32)
        gl = Rf[0:16, 0:1280].rearrange("p (a b) -> p a b", b=128)
        gcols = sb("gcols", [128, 10, 16], F32)
        lbt = sb("lbt", [128, 8, 16], F32)
        qg = sb("qg", [128, 1], F32)
        kg_bc = sb("kg_bc", [128, 64], F32)
        esink = sb("esink", [128, 32], F32)
        ksm = sb("ksm", [128, 2, 4], F32)
        kcar = sb("kcar", [128, 4, 128], BF16)
        vcar = sb("vcar", [128, 4, 128], BF16)

        ps = [st.enter_context(nc.psum_tensor(f"ps{i}", [128, 512], F32)) for i in range(8)]
        pools = {"mm": [0, 1, 2, 3], "aux": [4, 5], "g": [6, 7], "g0": [6, 2], "g1": [7, 3], "mm01": [0, 1]}
        pool_ctr = {k: 0 for k in pools}

        def bank(pool):
            i = pools[pool][pool_ctr[pool] % len(pools[pool])]
            pool_ctr[pool] += 1
            return i

        def pk(i):
            return [("ps", i)]

        def rk(region, lo, hi):
            return [(region, p) for p in range(lo // 512, (hi + 511) // 512)]

        wctr = [0]
        wtile = [0, 0]

        def wload(parts):
            s = wctr[0] % NSLOT
            wctr[0] += 1
            idx = wtile[0]
            wtile[0] += 1
            assert idx < NSTR
            off = 0
            views, keys = [], [("Wslot", s)]
            for pi, (view, kc, n) in enumerate(parts):
                views.append(W[s][:, off:off + kc * n].rearrange("p (k n) -> p k n", n=n))
                keys.append(("W", s, pi))
                off += kc * n
            assert off <= 8192
            ti_ = wtile[1]
            cast = ti_ == 0 or (ti_ == 1 and idx % 2 == 1)
            wback = (ti_ == 0 and idx % 2 == 0) or (ti_ == 1 and idx % 2 == 1)
            if cast:
                for pi, (view, kc, n) in enumerate(parts):
                    wr = [("W", s, pi)] + ([("Wslot", s)] if pi == 0 else [])
                    P.dma(POOL, lambda e, dst=views[pi], view=view: e.dma_start(out=dst, in_=view),
                          writes=wr, sem_key=("W", s, pi))
                if wback:
                    P.dma(SP, lambda e, idx=idx, s=s, off=off: e.dma_start(out=wscr[idx][:, 0:off], in_=W[s][:, 0:off]),
                          reads=keys, writes=[("scr", idx)], sem_key=("wb", s))
            else:
                P.dma(POOL, lambda e, idx=idx, s=s, off=off: e.dma_start(out=W[s][:, 0:off], in_=wscr[idx][:, 0:off]),
                      reads=[("scr", idx)], writes=keys, sem_key=("W", s, 0))
            return views, keys

        w_in_v = w_in.rearrange("(kc p) n -> p kc n", p=128)
        a_wo_v = a_wo.rearrange("(kc p) n -> p kc n", p=128)
        wqkv_v = wqkv.rearrange("(kc p) n -> p kc n", p=128)
        b_wo_v = b_wo.rearrange("(kc p) n -> p kc n", p=128)
        wgu_v = [wgu[l].rearrange("(kc p) n -> p kc n", p=128) for l in range(2)]
        wdown_v = [wdown[l].rearrange("(kc p) n -> p kc n", p=128) for l in range(2)]
        wpp_v = [wpp[l].rearrange("(kc p) n -> p kc n", p=128) for l in range(2)]
        wpg_v = [wpg[l].rearrange("(kc p) n -> p kc n", p=128) for l in range(2)]

        P.op(POOL, lambda e: e.memset(sstate[:], 0.0), writes=[("S", i) for i in range(16)])
        P.op(POOL, lambda e: e.memset(ident[:], 0.0), writes=["ident"])
        P.op(POOL, lambda e: e.affine_select(out=ident[:], in_=ident[:], pattern=[[-1, 128]], base=0,
                                             channel_multiplier=1, compare_op=ALU.not_equal, fill=1.0),
             reads=["ident"], writes=["ident"])
        P.op(POOL, lambda e: e.memset(ones_bf[:], 1.0), writes=["ones"])
        P.op(POOL, lambda e: e.memset(bd_bf[:], 0.0), writes=["bd"])
        P.op(POOL, lambda e: e.memset(bd_bf[0:64, 0:64], 1.0), writes=["bd"])
        P.op(POOL, lambda e: e.memset(bd_bf[64:128, 64:128], 1.0), writes=["bd"])
        P.op(POOL, lambda e: e.memset(scanmask[:], 1.0), writes=["scanmask"])
        P.op(POOL, lambda e: e.memset(scanmask[:].rearrange("p (c t) -> p c t", t=64)[:, :, 0:1], 0.0),
             writes=["scanmask"])
        P.op(POOL, lambda e: e.memset(bdmask[:], 1.0), writes=["bdmask"])
        P.op(POOL, lambda e: e.affine_select(out=bdmask[:], in_=bdmask[:], pattern=[[1, 128]], base=0,
                                             channel_multiplier=-1, compare_op=ALU.is_ge, fill=0.0),
             reads=["bdmask"], writes=["bdmask"])
        P.op(POOL, lambda e: e.memset(bdmask[0:64, 64:128], 0.0), writes=["bdmask"])
        P.op(POOL, lambda e: e.iota(D0[:], pattern=[[1, 128]], base=128, channel_multiplier=-1,
                                    allow_small_or_imprecise_dtypes=True), writes=["D0"])
        P.op(POOL, lambda e: e.memset(D0[0:64, 64:128], DMASK), writes=["D0"])
        P.op(POOL, lambda e: e.iota(D1[:], pattern=[[1, 128]], base=0, channel_multiplier=-1,
                                    allow_small_or_imprecise_dtypes=True), writes=["D1"])
        P.op(DVE, lambda e: e.tensor_scalar(out=D1s[:], in0=D1[:], scalar1=-1.0, scalar2=None, op0=ALU.mult),
             reads=["D1"], writes=["D1s"])
        P.op(DVE, lambda e: e.tensor_tensor(out=D1[:], in0=D1[:], in1=D1s[:], op=ALU.max),
             reads=["D1", "D1s"], writes=["D1"])
        P.op(POOL, lambda e: e.memset(D1[64:128, 0:64], DMASK), writes=["D1"])
        P.op(POOL, lambda e: e.tensor_copy(out=D1s[:], in_=D1[:]), reads=["D1"], writes=["D1s"])
        P.op(POOL, lambda e: e.memset(D1s[0:64, 64:128], DMASK), writes=["D1s"])
        vecs = [nmix[0], nmix[1], nffn[0], nffn[1], nple[0], nple[1], gnorm[0], lbl[0], lbl[1], lbl[2]]
        for i, v in enumerate(vecs):
            P.dma(SP, lambda e, i=i, v=v: e.dma_start(out=gl[:, i, :], in_=v.rearrange("(k p) -> k p", p=128)),
                  writes=[("R", 0)], sem_key=("gl", i))
        bg = bank("aux")
        for i in range(10):
            P.op(PE, lambda e, i=i: e.transpose(out=ps[bg][:, 16 * i:16 * i + 16], in_=gl[:, i, :],
                                                identity=ident[0:16, 0:16]),
                 reads=[("R", 0), "ident"], writes=pk(bg))
        P.op(DVE, lambda e: e.tensor_copy(out=gcols[:].rearrange("p a b -> p (a b)"), in_=ps[bg][:, 0:160]),
             reads=pk(bg), writes=["gcols"])
        P.op(ACT, lambda e: e.activation(out=lbt[:, 0:3, :], in_=gcols[:, 7:10, :], func=AF.Exp),
             reads=["gcols"], writes=["lbt"])
        P.op(DVE, lambda e: e.tensor_tensor(out=lbt[:, 3, :], in0=lbt[:, 0, :], in1=lbt[:, 1, :], op=ALU.add),
             reads=["lbt"], writes=["lbt"])
        P.op(DVE, lambda e: e.tensor_tensor(out=lbt[:, 3, :], in0=lbt[:, 3, :], in1=lbt[:, 2, :], op=ALU.add),
             reads=["lbt"], writes=["lbt"])
        P.op(DVE, lambda e: e.reciprocal(out=lbt[:, 4, :], in_=lbt[:, 3, :]), reads=["lbt"], writes=["lbt"])
        P.op(DVE, lambda e: e.tensor_tensor(out=lbt[:, 5, :], in0=lbt[:, 0, :], in1=lbt[:, 4, :], op=ALU.mult),
             reads=["lbt"], writes=["lbt"])
        P.op(DVE, lambda e: e.tensor_scalar(out=lbt[:, 6, :], in0=lbt[:, 5, :], scalar1=-1.0, scalar2=1.0,
                                            op0=ALU.mult, op1=ALU.add), reads=["lbt"], writes=["lbt"])
        P.op(ACT, lambda e: e.activation(out=lbt[:, 7, :], in_=lbt[:, 6, :], func=AF.Ln),
             reads=["lbt"], writes=["lbt"])
        for hf in range(2):
            P.dma(SP, lambda e, hf=hf: e.dma_start(out=qg[64 * hf:64 * hf + 64, :],
                                                   in_=qnorm.rearrange("o d -> d o")),
                  writes=["qg"], sem_key=("qg", hf))
        P.dma(SP, lambda e: e.dma_start(out=kg_bc[:], in_=knorm[0].partition_broadcast(128)),
              writes=["kg"], sem_key="kg")
        P.dma(SP, lambda e: e.dma_start(out=esink[:], in_=sinks[0].partition_broadcast(128)),
              writes=["esink"], sem_key="esink")
        P.op(ACT, lambda e: e.activation(out=esink[:], in_=esink[:], func=AF.Exp),
             reads=["esink"], writes=["esink"])

        evac_ctr = [0]

        def evac(out, in_, reads, writes):
            evac_ctr[0] += 1
            if evac_ctr[0] % 2:
                P.op(ACT, lambda e: e.activation(out=out, in_=in_, func=AF.Copy), reads=reads, writes=writes)
            else:
                P.op(DVE, lambda e: e.tensor_copy(out=out, in_=in_), reads=reads, writes=writes)

        def hk(kc):
            return [("h", kc)]

        def xk(kc):
            return [("xn", kc)]

        ALLX = [("xn", kc) for kc in range(KC)]

        def load_x(tile, T):
            NB = T // 128
            for b in range(NB):
                src = xp[tile["t0"] + 128 * b: tile["t0"] + 128 * b + 128, :] if tile["kind"] == "p" \
                    else xs[128 * b:128 * b + 128, :]
                P.dma(SP, lambda e, b=b, src=src: e.dma_start(out=Rf[:, 2048 * b:2048 * b + 2048], in_=src),
                      writes=rk("R", 4096 * b, 4096 * b + 4096), sem_key=("xin", b))
            for kc in range(KC):
                bk = bank("aux")
                for b in range(NB):
                    P.op(PE, lambda e, b=b, kc=kc, bk=bk: e.transpose(
                        out=ps[bk][:, 128 * b:128 * b + 128],
                        in_=Rf[:, 2048 * b + 128 * kc:2048 * b + 128 * kc + 128], identity=ident[:]),
                        reads=rk("R", 4096 * b, 4096 * b + 4096) + ["ident"], writes=pk(bk))
                evac(h[:, kc, 0:T], ps[bk][:, 0:T], pk(bk), hk(kc))

        def store_y(tile, T):
            NB = T // 128
            for b in range(NB):
                for g4 in range(4):
                    bk = bank("aux")
                    for k4 in range(4):
                        kc = 4 * g4 + k4
                        P.op(PE, lambda e, b=b, kc=kc, k4=k4, bk=bk: e.transpose(
                            out=ps[bk][:, 128 * k4:128 * k4 + 128], in_=h[:, kc, 128 * b:128 * b + 128],
                            identity=ident[:]), reads=hk(kc) + ["ident"], writes=pk(bk))
                    lo = 2048 * b + 512 * g4
                    evac(Rf[:, lo:lo + 512], ps[bk][:, :], pk(bk), rk("R", 2 * lo, 2 * lo + 1024))
                dst = yp[tile["t0"] + 128 * b: tile["t0"] + 128 * b + 128, :] if tile["kind"] == "p" \
                    else ys[128 * b:128 * b + 128, :]
                P.dma(SP, lambda e, b=b, dst=dst: e.dma_start(out=dst, in_=Rf[:, 2048 * b:2048 * b + 2048]),
                      reads=rk("R", 4096 * b, 4096 * b + 4096), sem_key=("yout", b), is_output=True)

        def rms_stats(src_fn, src_keys_fn, T, inv_n):
            bk = bank("aux")
            for kc in range(KC):
                s = sq[kc % 2]
                P.op(ACT, lambda e, kc=kc, s=s: e.activation(out=s[:, 0:T], in_=src_fn(kc), func=AF.Square),
                     reads=src_keys_fn(kc), writes=[("sq", kc % 2)])
                P.op(PE, lambda e, kc=kc, s=s, bk=bk: e.matmul(ps[bk][:, 0:T], lhsT=ones_bf[:], rhs=s[:, 0:T],
                                                              start=(kc == 0), stop=(kc == KC - 1)),
                     reads=[("sq", kc % 2), "ones"], writes=pk(bk))
            P.op(ACT, lambda e, bk=bk: e.activation(out=rstd[:, 0:T], in_=ps[bk][:, 0:T], func=AF.Ln,
                                                    scale=inv_n, bias=EPS), reads=pk(bk), writes=["rstd"])
            P.op(ACT, lambda e: e.activation(out=rstd[:, 0:T], in_=rstd[:, 0:T], func=AF.Exp, scale=-0.5),
                 reads=["rstd"], writes=["rstd"])

        def rmsnorm_h(gi, T):
            rms_stats(lambda kc: h[:, kc, 0:T], hk, T, 1.0 / D)
            for kc in range(KC):
                P.op(DVE, lambda e, kc=kc: e.scalar_tensor_tensor(
                    out=xn[:, kc, 0:T], in0=h[:, kc, 0:T], scalar=gcols[:, gi, kc:kc + 1], in1=rstd[:, 0:T],
                    op0=ALU.mult, op1=ALU.mult), reads=hk(kc) + ["gcols", "rstd"], writes=xk(kc))

        def proj_fm(wv, wkeys, col0, rhs_fn, rhs_keys, nk, T, pool="mm"):
            bk = bank(pool)
            for k in range(nk):
                P.op(PE, lambda e, k=k, bk=bk: e.matmul(ps[bk][:, 0:T], lhsT=wv[:, k, col0:col0 + 128],
                                                       rhs=rhs_fn(k), start=(k == 0), stop=(k == nk - 1)),
                     reads=wkeys + rhs_keys(k), writes=pk(bk))
            return bk

        def resid_add(oc, bk, T):
            P.op(DVE, lambda e: e.tensor_tensor(out=h[:, oc, 0:T], in0=ps[bk][:, 0:T], in1=h[:, oc, 0:T],
                                                op=ALU.add), reads=pk(bk) + hk(oc), writes=hk(oc))

        def ffn(l, T):
            rmsnorm_h(2 + l, T)
            act = R[:, 0:FC * TMAX].rearrange("p (j t) -> p j t", t=TMAX)
            for s in range(22):
                (wg_, wu_), kgu_ = wload([(wgu_v[l][:, :, 256 * s:256 * s + 256], KC, 256),
                                          (wgu_v[l][:, :, FF + 256 * s:FF + 256 * s + 256], KC, 256)])
                for j4 in range(2):
                    j = 2 * s + j4
                    bg_ = proj_fm(wg_, kgu_, 128 * j4, lambda k: xn[:, k, 0:T], xk, KC, T)
                    bu_ = proj_fm(wu_, kgu_, 128 * j4, lambda k: xn[:, k, 0:T], xk, KC, T)
                    t_ = tmp[j % 2]
                    P.op(ACT, lambda e, t_=t_, bg_=bg_: e.activation(out=t_[:, 0:T], in_=ps[bg_][:, 0:T],
                                                                     func=AF.Silu),
                         reads=pk(bg_), writes=[("tmp", j % 2)])
                    P.op(DVE, lambda e, t_=t_, bu_=bu_, j=j: e.tensor_tensor(
                        out=act[:, j, 0:T], in0=ps[bu_][:, 0:T], in1=t_[:, 0:T], op=ALU.mult),
                        reads=pk(bu_) + [("tmp", j % 2)], writes=rk("R", j * TMAX, j * TMAX + TMAX))
            QK = FC // 4
            for og in range(KC // 4):
                bks = [bank("mm") for _ in range(4)]
                for qt in range(4):
                    (wd_,), kd_ = wload([(wdown_v[l][:, QK * qt:QK * qt + QK, 512 * og:512 * og + 512], QK, 512)])
                    for o4 in range(4):
                        for k in range(QK):
                            kk = QK * qt + k
                            P.op(PE, lambda e, k=k, kk=kk, o4=o4, qt=qt, wd_=wd_, bk=bks[o4]: e.matmul(
                                ps[bk][:, 0:T], lhsT=wd_[:, k, 128 * o4:128 * o4 + 128], rhs=act[:, kk, 0:T],
                                start=(qt == 0 and k == 0), stop=(qt == 3 and k == QK - 1)),
                                reads=kd_ + rk("R", kk * TMAX, kk * TMAX + TMAX), writes=pk(bks[o4]))
                for o4 in range(4):
                    resid_add(4 * og + o4, bks[o4], T)

        def ple(l, tile, T):
            NB = T // 128
            rmsnorm_h(4 + l, T)
            for b in range(NB):
                src = pp[l, tile["t0"] + 128 * b: tile["t0"] + 128 * b + 128, :] if tile["kind"] == "p" \
                    else psm[l, 128 * b:128 * b + 128, :]
                P.dma(SP, lambda e, b=b, src=src: e.dma_start(out=Qf[:, 256 * b:256 * b + 256], in_=src),
                      writes=rk("Q", 512 * b, 512 * b + 512), sem_key=("pin", b))
            for k2 in range(2):
                bk = bank("aux")
                for b in range(NB):
                    P.op(PE, lambda e, b=b, k2=k2, bk=bk: e.transpose(
                        out=ps[bk][:, 128 * b:128 * b + 128],
                        in_=Qf[:, 256 * b + 128 * k2:256 * b + 128 * k2 + 128], identity=ident[:]),
                        reads=rk("Q", 512 * b, 512 * b + 512) + ["ident"], writes=pk(bk))
                evac(pT[:, k2, 0:T], ps[bk][:, 0:T], pk(bk), [("pT", k2)])
            for s in range(8):
                (wg_, wpp_), kg_ = wload([(wpg_v[l][:, :, 256 * s:256 * s + 256], KC, 256),
                                          (wpp_v[l][:, :, 256 * s:256 * s + 256], 2, 256)])
                for o4 in range(2):
                    oc = 2 * s + o4
                    bg_ = proj_fm(wg_, kg_, 128 * o4, lambda k: xn[:, k, 0:T], xk, KC, T)
                    bp_ = proj_fm(wpp_, kg_, 128 * o4, lambda k: pT[:, k, 0:T], lambda k: [("pT", k)], 2, T)
                    t_ = tmp[oc % 2]
                    P.op(ACT, lambda e, t_=t_, bg_=bg_: e.activation(out=t_[:, 0:T], in_=ps[bg_][:, 0:T],
                                                                     func=AF.Sigmoid),
                         reads=pk(bg_), writes=[("tmp", oc % 2)])
                    P.op(DVE, lambda e, t_=t_, bp_=bp_: e.tensor_tensor(
                        out=t_[:, 0:T], in0=ps[bp_][:, 0:T], in1=t_[:, 0:T], op=ALU.mult),
                        reads=pk(bp_) + [("tmp", oc % 2)], writes=[("tmp", oc % 2)])
                    P.op(DVE, lambda e, t_=t_, oc=oc: e.tensor_tensor(
                        out=h[:, oc, 0:T], in0=h[:, oc, 0:T], in1=t_[:, 0:T], op=ALU.add),
                        reads=hk(oc) + [("tmp", oc % 2)], writes=hk(oc))

        def hgrn(tile, T):
            NB = T // 128
            NCH = T // 64
            is_s = tile["kind"] == "s"
            rmsnorm_h(0, T)
            hstop = (dbg or {}).get("hstop", 99)
            if hstop == 0:
                return
            o_sb = Rf[:, 0:KC * TMAX].rearrange("p (k t) -> p k t", t=TMAX)
            v_sb = R[:, 16384:16384 + 4096].rearrange("p (b f) -> p b f", f=1024)
            q1 = Q[:, 0:4096].rearrange("p (a t) -> p a t", t=TMAX)
            k2 = Q[:, 4096:8192].rearrange("p (a t) -> p a t", t=TMAX)
            k3T = Q[:, 8192:12288].rearrange("p (a b d) -> p a b d", b=4, d=128)

            def okeys(hh):
                return rk("R", 2 * hh * TMAX, 2 * hh * TMAX + 2 * TMAX)

            def q1k(a):
                return rk("Q", a * TMAX, a * TMAX + TMAX)

            def k2k(a):
                return rk("Q", 4096 + a * TMAX, 4096 + a * TMAX + TMAX)

            def k3k(a):
                return rk("Q", 8192 + a * TMAX, 8192 + a * TMAX + TMAX)

            VK = rk("R", 16384, 16384 + 4096)

            lastp = (not is_s) and tile["t0"] + T == SEQ

            for hg in range(2):
                ctx = {}

                def V():
                    for s2 in range(2):
                        c0 = 4096 + (hg * 8 + s2 * 4) * 128
                        (wv_,), kv_ = wload([(w_in_v[:, :, c0:c0 + 512], KC, 512)])
                        for b in range(NB):
                            bk = bank("mm")
                            for k in range(KC):
                                P.op(PE, lambda e, k=k, b=b, bk=bk, wv_=wv_: e.matmul(
                                    ps[bk][:, :], lhsT=xn[:, k, 128 * b:128 * b + 128], rhs=wv_[:, k, :],
                                    start=(k == 0), stop=(k == KC - 1)), reads=kv_ + xk(k), writes=pk(bk))
                            evac(v_sb[:, b, 512 * s2:512 * s2 + 512], ps[bk][:, :], pk(bk), VK)

                def Pj(a):
                    if a % 2 == 0:
                        cq = (hg * 8 + a) * 128
                        ctx["w"] = wload([(w_in_v[:, :, cq:cq + 256], KC, 256),
                                          (w_in_v[:, :, 2048 + cq:2048 + cq + 256], KC, 256)])
                    (wq_, wf_), kq_ = ctx["w"]
                    hh = a % 2
                    bq = proj_fm(wq_, kq_, 128 * hh, lambda k: xn[:, k, 0:T], xk, KC, T)
                    bf = proj_fm(wf_, kq_, 128 * hh, lambda k: xn[:, k, 0:T], xk, KC, T)
                    ctx[a] = (bq, bf)

                def tms(a):
                    p4 = 4 * (a % 2)
                    return [tmp[p4 + i][:, 0:T] for i in range(4)], [("tmp", p4 + i) for i in range(4)]

                def S1(a):
                    bq, bf = ctx[a]
                    hd = hg * 8 + a
                    (t0_, t1_, t2_, t3_), TK = tms(a)
                    qps, fps = ps[bq][:, 0:T], ps[bf][:, 0:T]
                    lbc = lbt[:, 5, hd:hd + 1]
                    P.op(ACT, lambda e: e.activation(out=t0_, in_=qps, func=AF.Exp, scale=-1.0),
                         reads=pk(bq), writes=[TK[0]])
                    P.op(ACT, lambda e: e.activation(out=t0_, in_=t0_, func=AF.Ln, bias=1.0),
                         reads=[TK[0]], writes=[TK[0]])
                    P.op(ACT, lambda e: e.activation(out=t1_, in_=fps, func=AF.Exp, scale=-1.0),
                         reads=pk(bf), writes=[TK[1]])
                    P.op(ACT, lambda e: e.activation(out=t2_, in_=t1_, func=AF.Ln, scale=lbc, bias=1.0),
                         reads=[TK[1], "lbt"], writes=[TK[2]])
                    P.op(ACT, lambda e: e.activation(out=t1_, in_=t1_, func=AF.Ln, bias=1.0),
                         reads=[TK[1]], writes=[TK[1]])

                def chain_steps(a, c0_, c1_):
                    hd = hg * 8 + a
                    par = a % 2
                    ub = ctx[("ub", a)]
                    Sb = [sstate[:, hd, :], sx[:, par, :]]
                    SbK = [("S", hd), ("sx", par)]
                    for c in range(c0_, c1_):
                        U = ps[ub[c % 2]][:, 128 * (c // 2):128 * (c // 2) + 128]
                        dcol = egl[:, a, c:c + 1]
                        src, dst = Sb[c % 2], Sb[(c + 1) % 2]
                        P.op(ACT, lambda e, src=src, c=c: e.activation(out=sbf[:, par, c, :], in_=src, func=AF.Copy),
                             reads=[SbK[c % 2]], writes=[("sbf", par, c)])
                        P.op(DVE, lambda e, src=src, dst=dst, U=U, dcol=dcol: e.scalar_tensor_tensor(
                            out=dst, in0=src, scalar=dcol, in1=U, op0=ALU.mult, op1=ALU.add),
                            reads=[SbK[c % 2], ("egl", a)] + pk(ub[c % 2]), writes=[SbK[(c + 1) % 2]])
                    if c1_ == NCH and lastp:
                        P.dma(SP, lambda e: e.dma_start(out=sp_out[hd], in_=Sb[0]), reads=[SbK[0]],
                              sem_key=("spo", hd % 4), is_output=True)

                def sample_states(a):
                    hd = hg * 8 + a
                    par = a % 2
                    ub = ctx[("ub", a)]
                    ip = a % 2
                    Sin = sstate[:, 4 * ip:4 * ip + 4, :]
                    Sout = sstate[:, 8 + 4 * ip:8 + 4 * ip + 4, :]
                    SKi = [("S", 4 * ip + j) for j in range(4)]
                    SKo = [("S", 8 + 4 * ip + j) for j in range(4)]
                    P.dma(SP, lambda e: e.dma_start(out=Sin, in_=st_in[:, hd].rearrange("j k v -> k j v")),
                          writes=SKi, sem_key=("sin", ip))
                    P.op(ACT, lambda e: e.activation(out=sbf[:, par, 0:4, :], in_=Sin, func=AF.Copy),
                         reads=SKi, writes=[("sbf", par, c) for c in range(NCH)])
                    for c in range(NCH):
                        U = ps[ub[c % 2]][:, 128 * (c // 2):128 * (c // 2) + 128]
                        dcol = egl[:, a, c:c + 1]
                        P.op(DVE, lambda e, c=c, U=U, dcol=dcol: e.scalar_tensor_tensor(
                            out=Sout[:, c, :], in0=Sin[:, c, :], scalar=dcol, in1=U, op0=ALU.mult, op1=ALU.add),
                            reads=SKi + pk(ub[c % 2]) + [("egl", a)], writes=[SKo[c]])
                    P.dma(SP, lambda e: e.dma_start(out=ss_out[:, hd].rearrange("j k v -> k j v"), in_=Sout),
                          reads=SKo, sem_key=("sout", ip), is_output=True)

                def side(prev, part):
                    if prev is None:
                        return
                    if is_s:
                        if part == 0:
                            sample_states(prev)
                        return
                    bounds = [0, NCH // 3 + 1, 2 * NCH // 3 + 1, NCH]
                    chain_steps(prev, bounds[part], bounds[part + 1])

                def mask_A(a):
                    par = a % 2
                    ba = ctx[("ba", a)]
                    A_sb = sq[par][:, 0:T].rearrange("p (b t) -> p b t", t=128)
                    P.op(DVE, lambda e: e.tensor_tensor(
                        out=A_sb, in0=ps[ba][:, 0:T].rearrange("p (b t) -> p b t", t=128),
                        in1=bdmask[:].unsqueeze(1).to_broadcast([128, NB, 128]), op=ALU.mult),
                        reads=pk(ba) + ["bdmask"], writes=[("sq", par)])

                def B1pe(a):
                    ub = [bank("g"), bank("g")]
                    ctx[("ub", a)] = ub
                    for c in range(NCH):
                        b, hf = c // 2, c % 2
                        P.op(PE, lambda e, b=b, hf=hf, ubk=ub[hf]: e.matmul(
                            ps[ubk][:, 128 * b:128 * b + 128], lhsT=k3T[64 * hf:64 * hf + 64, a, b, :],
                            rhs=v_sb[64 * hf:64 * hf + 64, b, 128 * a:128 * a + 128], start=True, stop=True),
                            reads=k3k(a) + VK, writes=pk(ub[hf]))
                    ba = bank("aux")
                    ctx[("ba", a)] = ba
                    for b in range(NB):
                        P.op(PE, lambda e, b=b: e.matmul(
                            ps[ba][:, 128 * b:128 * b + 128], lhsT=k2[:, a, 128 * b:128 * b + 128],
                            rhs=q1[:, a, 128 * b:128 * b + 128], start=True, stop=True),
                            reads=k2k(a) + q1k(a), writes=pk(ba))

                def S2to6(a, prev, pre_t=None):
                    bq, bf = ctx[a]
                    hd = hg * 8 + a
                    (t0_, t1_, t2_, t3_), TK = tms(a)
                    qps, fps = ps[bq][:, 0:T], ps[bf][:, 0:T]
                    ln1m = lbt[:, 7, hd:hd + 1]
                    P.op(DVE, lambda e: e.tensor_tensor(out=t2_, in0=t2_, in1=t1_, op=ALU.subtract),
                         reads=[TK[1], TK[2]], writes=[TK[2]])
                    P.op(DVE, lambda e: e.tensor_tensor_scan(out=t3_, data0=scanmask[:, 0:T], data1=t2_, initial=0.0,
                                                             op0=ALU.mult, op1=ALU.add),
                         reads=[TK[2], "scanmask"], writes=[TK[3]])
                    P.op(DVE, lambda e: e.tensor_tensor(out=t0_, in0=t3_, in1=t0_, op=ALU.subtract),
                         reads=[TK[0], TK[3]], writes=[TK[0]])
                    P.op(ACT, lambda e: e.activation(out=t0_, in_=t0_, func=AF.Exp), reads=[TK[0]], writes=[TK[0]])
                    side(prev, 0)
                    P.op(DVE, lambda e: e.tensor_tensor(out=q1[:, a, 0:T], in0=qps, in1=t0_, op=ALU.mult),
                         reads=pk(bq) + [TK[0]], writes=q1k(a))
                    P.op(DVE, lambda e: e.tensor_tensor(out=t1_, in0=fps, in1=t1_, op=ALU.add),
                         reads=pk(bf) + [TK[1]], writes=[TK[1]])
                    P.op(DVE, lambda e: e.tensor_tensor(out=t1_, in0=t1_, in1=t3_, op=ALU.add),
                         reads=[TK[1], TK[3]], writes=[TK[1]])
                    P.op(ACT, lambda e: e.activation(out=t2_, in_=t1_, func=AF.Exp, scale=-1.0, bias=ln1m),
                         reads=[TK[1], "lbt"], writes=[TK[2]])
                    P.op(ACT, lambda e: e.activation(
                        out=egl[:, a, 0:NCH], in_=t3_.rearrange("p (c t) -> p c t", t=64)[:, :, 63], func=AF.Exp),
                        reads=[TK[3]], writes=[("egl", a)])
                    P.op(ACT, lambda e: e.activation(out=k2[:, a, 0:T], in_=t1_, func=AF.Exp, scale=-1.0, bias=ln1m),
                         reads=[TK[1], "lbt"], writes=k2k(a))
                    side(prev, 1)
                    P.op(DVE, lambda e: e.tensor_tensor(
                        out=t2_.rearrange("p (c t) -> p c t", t=64), in0=t2_.rearrange("p (c t) -> p c t", t=64),
                        in1=egl[:, a, 0:NCH].unsqueeze(2).to_broadcast([128, NCH, 64]), op=ALU.mult),
                        reads=[TK[2], ("egl", a)], writes=[TK[2]])
                    side(prev, 2)
                    if prev is not None:
                        mask_A(prev)
                    if pre_t is not None:
                        pre_t()
                    bt = bank("aux")
                    for b in range(NB):
                        P.op(PE, lambda e, b=b: e.transpose(out=ps[bt][:, 128 * b:128 * b + 128],
                                                            in_=t2_[:, 128 * b:128 * b + 128], identity=ident[:]),
                             reads=[TK[2], "ident"], writes=pk(bt))
                    evac(k3T[:, a, 0:NB, :], ps[bt][:, 0:T].rearrange("p (b d) -> p b d", d=128), pk(bt), k3k(a))

                def B2(a):
                    hd = hg * 8 + a
                    par = a % 2
                    A_sb = sq[par][:, 0:T].rearrange("p (b t) -> p b t", t=128)
                    bo = bank("aux")
                    for b in range(NB):
                        P.op(PE, lambda e, b=b: e.matmul(
                            ps[bo][:, 128 * b:128 * b + 128], lhsT=v_sb[:, b, 128 * a:128 * a + 128],
                            rhs=A_sb[:, b, :], start=True, stop=False), reads=VK + [("sq", par)], writes=pk(bo))
                        for hf in range(2):
                            c = 2 * b + hf
                            P.op(PE, lambda e, b=b, hf=hf, c=c: e.matmul(
                                ps[bo][:, 128 * b + 64 * hf:128 * b + 64 * hf + 64], lhsT=sbf[:, par, c, :],
                                rhs=q1[:, a, 128 * b + 64 * hf:128 * b + 64 * hf + 64], start=False, stop=(hf == 1)),
                                reads=[("sbf", par, c)] + q1k(a), writes=pk(bo))
                    evac(o_sb[:, hd, 0:T], ps[bo][:, 0:T], pk(bo), okeys(hd))

                V()
                if hstop == 1:
                    return
                Pj(0)
                S1(0)
                for a in range(8):
                    if a >= 1:
                        B1pe(a - 1)
                    if a + 1 < 8:
                        Pj(a + 1)
                    S2to6(a, a - 1 if a >= 1 else None, (lambda a=a: B2(a - 2)) if a >= 2 else None)
                    if a + 1 < 8:
                        S1(a + 1)
                B1pe(7)
                for part in range(3):
                    side(7, part)
                mask_A(7)
                B2(6)
                B2(7)
            if hstop == 4:
                return
            rms_stats(lambda kc: o_sb[:, kc, 0:T], okeys, T, 1.0 / D)
            y = Q[:, 0:KC * TMAX].rearrange("p (k t) -> p k t", t=TMAX)

            def yk(kc):
                return rk("Q", kc * TMAX, kc * TMAX + TMAX)

            for s in range(4):
                (wg_,), kg_ = wload([(w_in_v[:, :, 6144 + 512 * s:6144 + 512 * s + 512], KC, 512)])
                for hh in range(4):
                    hd = 4 * s + hh
                    bg_ = proj_fm(wg_, kg_, 128 * hh, lambda k: xn[:, k, 0:T], xk, KC, T)
                    ta, tb = tmp[2 * (hd % 2)], tmp[2 * (hd % 2) + 1]
                    ka, kb_ = ("tmp", 2 * (hd % 2)), ("tmp", 2 * (hd % 2) + 1)
                    P.op(ACT, lambda e, ta=ta, bg_=bg_: e.activation(out=ta[:, 0:T], in_=ps[bg_][:, 0:T], func=AF.Silu),
                         reads=pk(bg_), writes=[ka])
                    P.op(DVE, lambda e, tb=tb, hd=hd: e.tensor_tensor(out=tb[:, 0:T], in0=o_sb[:, hd, 0:T],
                                                                      in1=rstd[:, 0:T], op=ALU.mult),
                         reads=okeys(hd) + ["rstd"], writes=[kb_])
                    P.op(DVE, lambda e, ta=ta, tb=tb, hd=hd: e.scalar_tensor_tensor(
                        out=y[:, hd, 0:T], in0=tb[:, 0:T], scalar=gcols[:, 6, hd:hd + 1], in1=ta[:, 0:T],
                        op0=ALU.mult, op1=ALU.mult), reads=[ka, kb_, "gcols"], writes=yk(hd))
            for s in range(4):
                (wo_,), ko_ = wload([(a_wo_v[:, :, 512 * s:512 * s + 512], KC, 512)])
                for o4 in range(4):
                    oc = 4 * s + o4
                    bk = proj_fm(wo_, ko_, 128 * o4, lambda k: y[:, k, 0:T], yk, KC, T)
                    resid_add(oc, bk, T)

        def swa(tile, T, first_prompt_tile):
            NB = T // 128
            is_s = tile["kind"] == "s"
            rmsnorm_h(1, T)
            qn = R[:, 0:KC * TMAX].rearrange("p (k t) -> p k t", t=TMAX)
            kT = Q[:, 0:2560].rearrange("p (g t) -> p g t", t=640)
            vS = Q[:, 2560:5120].rearrange("p (b g f) -> p b g f", g=4, f=128)
            pTt = Q[:, 5120:6144].rearrange("p (k t) -> p k t", t=512)
            kfd = [Qf[:, 3072 + 512 * i:3072 + 512 * i + 512].rearrange("p (g r d) -> p g r d", r=2, d=64)
                   for i in range(2)]
            vf = [Qf[:, 4096 + 256 * i:4096 + 256 * i + 256] for i in range(2)]

            def qnk(kc):
                return rk("R", kc * TMAX, kc * TMAX + TMAX)

            def kTk(blk):
                return rk("Q", 0, 2560)

            def vSk(blk):
                return rk("Q", 2560, 5120)

            PTK = [("Q", 10), ("Q", 11)]
            KFD = [rk("Q", 6144 + 1024 * i, 6144 + 1024 * i + 1024) for i in range(2)]
            VF = [rk("Q", 8192 + 512 * i, 8192 + 512 * i + 512) for i in range(2)]

            if not is_s and not first_prompt_tile:
                P.op(DVE, lambda e: e.tensor_copy(out=kT[:, :, 0:128], in_=kcar[:]), reads=["kcar"], writes=kTk(0))
                P.op(DVE, lambda e: e.tensor_copy(out=vS[:, 0], in_=vcar[:]), reads=["vcar"], writes=vSk(0))
            for s in range(4):
                (wq_,), kq_ = wload([(wqkv_v[:, :, 512 * s:512 * s + 512], KC, 512)])
                for c4 in range(4):
                    pc = 4 * s + c4
                    bq = proj_fm(wq_, kq_, 128 * c4, lambda k: xn[:, k, 0:T], xk, KC, T)
                    s_ = sq[pc % 2]
                    P.op(ACT, lambda e, s_=s_, bq=bq: e.activation(out=s_[:, 0:T], in_=ps[bq][:, 0:T], func=AF.Square),
                         reads=pk(bq), writes=[("sq", pc % 2)])
                    bs = bank("aux")
                    P.op(PE, lambda e, s_=s_, bs=bs: e.matmul(ps[bs][:, 0:T], lhsT=bd_bf[:], rhs=s_[:, 0:T],
                                                             start=True, stop=True),
                         reads=[("sq", pc % 2), "bd"], writes=pk(bs))
                    t_ = tmp[pc % 2]
                    P.op(ACT, lambda e, t_=t_, bs=bs: e.activation(out=t_[:, 0:T], in_=ps[bs][:, 0:T], func=AF.Ln,
                                                                   scale=1.0 / 64, bias=EPS),
                         reads=pk(bs), writes=[("tmp", pc % 2)])
                    P.op(ACT, lambda e, t_=t_: e.activation(out=t_[:, 0:T], in_=t_[:, 0:T], func=AF.Exp, scale=-0.5),
                         reads=[("tmp", pc % 2)], writes=[("tmp", pc % 2)])
                    P.op(DVE, lambda e, t_=t_, bq=bq, pc=pc: e.scalar_tensor_tensor(
                        out=qn[:, pc, 0:T], in0=ps[bq][:, 0:T], scalar=qg[:, 0:1], in1=t_[:, 0:T],
                        op0=ALU.mult, op1=ALU.mult), reads=pk(bq) + [("tmp", pc % 2), "qg"], writes=qnk(pc))
            sstop = (dbg or {}).get("hstop", 99)
            if sstop == 10:
                return
            (wkv_,), kkv_ = wload([(wqkv_v[:, :, 2048:2560], KC, 512)])
            kv_deferred = [None]
            for b in range(NB):
                i2 = b % 2
                bk = bank("mm")
                for k in range(KC):
                    P.op(PE, lambda e, k=k, b=b, bk=bk: e.matmul(ps[bk][:, :], lhsT=xn[:, k, 128 * b:128 * b + 128],
                                                               rhs=wkv_[:, k, :], start=(k == 0), stop=(k == KC - 1)),
                         reads=kkv_ + xk(k), writes=pk(bk))
                kps = ps[bk][:, 0:256]
                vps = ps[bk][:, 256:512]
                if kv_deferred[0] is not None:
                    kv_deferred[0]()
                    kv_deferred[0] = None
                P.op(ACT, lambda e, i2=i2, vps=vps: e.activation(out=vf[i2], in_=vps, func=AF.Copy),
                     reads=pk(bk), writes=VF[i2])
                t_ = tmp[2 + i2]
                tk_ = ("tmp", 2 + i2)
                P.op(ACT, lambda e, t_=t_, kps=kps: e.activation(out=t_[:, 0:256], in_=kps, func=AF.Square),
                     reads=pk(bk), writes=[tk_])
                P.op(DVE, lambda e, b=b, vps=vps: e.tensor_copy(
                    out=vS[:, 1 + b].rearrange("p g (r d) -> p g r d", r=2),
                    in_=vps.rearrange("p (g d) -> p g d", d=64).unsqueeze(2).to_broadcast([128, 4, 2, 64])),
                    reads=pk(bk) + VF[i2] + [tk_], writes=vSk(1 + b))
                if sstop == 13:
                    continue
                P.op(DVE, lambda e, t_=t_, i2=i2: e.tensor_reduce(
                    out=ksm[:, i2, :], in_=t_[:, 0:256].rearrange("p (g d) -> p g d", d=64), axis=AX.X, op=ALU.add),
                    reads=[tk_], writes=[("ksm", i2)])
                P.op(ACT, lambda e, i2=i2: e.activation(out=ksm[:, i2, :], in_=ksm[:, i2, :], func=AF.Ln,
                                                        scale=1.0 / 64, bias=EPS),
                     reads=[("ksm", i2)], writes=[("ksm", i2)])
                P.op(ACT, lambda e, i2=i2: e.activation(out=ksm[:, i2, :], in_=ksm[:, i2, :], func=AF.Exp, scale=-0.5),
                     reads=[("ksm", i2)], writes=[("ksm", i2)])
                P.op(DVE, lambda e, t_=t_, kps=kps, i2=i2: e.tensor_tensor(
                    out=t_[:, 0:256].rearrange("p (g d) -> p g d", d=64),
                    in0=kps.rearrange("p (g d) -> p g d", d=64),
                    in1=ksm[:, i2, :].unsqueeze(2).to_broadcast([128, 4, 64]), op=ALU.mult),
                    reads=pk(bk) + [("ksm", i2)], writes=[tk_])
                for r in range(2):
                    P.op(DVE, lambda e, t_=t_, i2=i2, r=r: e.tensor_tensor(
                        out=kfd[i2][:, :, r, :], in0=t_[:, 0:256].rearrange("p (g d) -> p g d", d=64),
                        in1=kg_bc[:].unsqueeze(1).to_broadcast([128, 4, 64]), op=ALU.mult),
                        reads=[tk_, "kg"], writes=KFD[i2])
                if sstop == 14:
                    continue
                if is_s:
                    P.dma(SP, lambda e, b=b, i2=i2: e.dma_start(
                        out=cks[128 * b:128 * b + 128, :].rearrange("t (g d) -> t g d", d=64), in_=kfd[i2][:, :, 0, :]),
                        reads=KFD[i2], sem_key=("cko", i2), is_output=True)
                    P.dma(SP, lambda e, b=b, i2=i2: e.dma_start(out=cvs[128 * b:128 * b + 128, :], in_=vf[i2]),
                          reads=VF[i2], sem_key=("cvo", i2), is_output=True)
                elif tile["t0"] + T == SEQ and b == NB - 1:
                    P.dma(SP, lambda e, i2=i2: e.dma_start(
                        out=ckp[:, :].rearrange("t (g d) -> t g d", d=64), in_=kfd[i2][:, :, 0, :]),
                        reads=KFD[i2], sem_key=("cko", i2), is_output=True)
                    P.dma(SP, lambda e, i2=i2: e.dma_start(out=cvp[:, :], in_=vf[i2]),
                          reads=VF[i2], sem_key=("cvo", i2), is_output=True)
                if sstop == 15:
                    continue
                def ktrans(b=b, i2=i2):
                    bt = bank("aux")
                    for g in range(4):
                        P.op(PE, lambda e, g=g, bt=bt, i2=i2: e.transpose(
                            out=ps[bt][:, 128 * g:128 * g + 128], in_=kfd[i2][:, g].rearrange("p r d -> p (r d)"),
                            identity=ident[:]), reads=KFD[i2] + ["ident"], writes=pk(bt))
                    evac(kT[:, :, 128 + 128 * b:128 + 128 * b + 128],
                         ps[bt][:, :].rearrange("p (g t) -> p g t", t=128), pk(bt), kTk(1 + b))

                kv_deferred[0] = ktrans
            if kv_deferred[0] is not None:
                kv_deferred[0]()
            if not is_s:
                P.op(DVE, lambda e: e.tensor_copy(out=kcar[:], in_=kT[:, :, TMAX:TMAX + 128]), reads=kTk(4),
                     writes=["kcar"])
                P.op(DVE, lambda e: e.tensor_copy(out=vcar[:], in_=vS[:, 4]), reads=vSk(4), writes=["vcar"])
            if is_s:
                cstk = Rf[:, 4096:4096 + 1024].rearrange("p (j f) -> p j f", f=256)
                cstv = Rf[:, 5120:5120 + 1024].rearrange("p (j f) -> p j f", f=256)
                kTc = R[:, 12288:14336].rearrange("p (j g t) -> p j g t", g=4, t=128)
                vc = R[:, 14336:16384].rearrange("p (j g f) -> p j g f", g=4, f=128)
                CK, CV = rk("R", 8192, 10240), rk("R", 10240, 12288)
                KTC, VC = rk("R", 12288, 14336), rk("R", 14336, 16384)
                P.dma(SP, lambda e: e.dma_start(out=cstk, in_=ck_in.rearrange("j t f -> t j f")), writes=CK,
                      sem_key="ckin")
                P.dma(SP, lambda e: e.dma_start(out=cstv, in_=cv_in.rearrange("j t f -> t j f")), writes=CV,
                      sem_key="cvin")
                for j in range(NSEQ_S):
                    i2 = j % 2
                    P.op(DVE, lambda e, j=j, i2=i2: e.tensor_copy(
                        out=kfd[i2][:], in_=cstk[:, j, :].rearrange("p (g d) -> p g d", d=64).unsqueeze(2)
                        .to_broadcast([128, 4, 2, 64])), reads=CK, writes=KFD[i2])
                    bt = bank("aux")
                    for g in range(4):
                        P.op(PE, lambda e, g=g, bt=bt, i2=i2: e.transpose(
                            out=ps[bt][:, 128 * g:128 * g + 128], in_=kfd[i2][:, g].rearrange("p r d -> p (r d)"),
                            identity=ident[:]), reads=KFD[i2] + ["ident"], writes=pk(bt))
                    evac(kTc[:, j], ps[bt][:, :].rearrange("p (g t) -> p g t", t=128), pk(bt), KTC)
                    P.op(DVE, lambda e, j=j: e.tensor_copy(
                        out=vc[:, j].rearrange("p g (r d) -> p g r d", r=2),
                        in_=cstv[:, j, :].rearrange("p (g d) -> p g d", d=64).unsqueeze(2)
                        .to_broadcast([128, 4, 2, 64])), reads=CV, writes=VC)
            if sstop in (11, 13, 14, 15):
                return
            attn_o = xn
            groups = []
            if is_s:
                for j in range(NSEQ_S):
                    b = j // 2
                    c0 = 64 * j
                    groups.append(dict(c0=c0, nt=64, kbs=[
                        dict(kT=lambda g, j=j: kTc[:, j, g, :], v=lambda g, j=j: vc[:, j, g, :],
                             Dm=D0[:, 0:64], keys=KTC + VC),
                        dict(kT=lambda g, b=b: kT[:, g, 128 + 128 * b:128 + 128 * b + 128],
                             v=lambda g, b=b: vS[:, 1 + b, g, :], Dm=D1s[:, 64 * (j % 2):64 * (j % 2) + 64],
                             keys=kTk(1 + b) + vSk(1 + b))]))
            else:
                for b in range(NB):
                    kbs = []
                    if not (first_prompt_tile and b == 0):
                        kbs.append(dict(kT=lambda g, b=b: kT[:, g, 128 * b:128 * b + 128],
                                        v=lambda g, b=b: vS[:, b, g, :], Dm=D0[:, :], keys=kTk(b) + vSk(b)))
                    kbs.append(dict(kT=lambda g, b=b: kT[:, g, 128 + 128 * b:128 + 128 * b + 128],
                                    v=lambda g, b=b: vS[:, 1 + b, g, :], Dm=D1[:, :], keys=kTk(1 + b) + vSk(1 + b)))
                    groups.append(dict(c0=128 * b, nt=128, kbs=kbs))
            pT2 = [pTt, Q[:, 9216:10240].rearrange("p (k t) -> p k t", t=512)]
            PT2K = [PTK, [("Q", 18), ("Q", 19)]]
            den_buf = [tmp[4], tmp[5]]
            den_key = [("tmp", 4), ("tmp", 5)]
            items = [(gr, g, par) for gr in groups for g in range(4) for par in range(2)]

            def stage1(it, part):
                gr, g, par = it
                c0, nt, kbs = gr["c0"], gr["nt"], gr["kbs"]
                N = 4 * nt
                lo, hi = 64 * par, 64 * par + 64
                rhs_q = qn[lo:hi, 4 * g:4 * g + 4, c0:c0 + nt]
                qkeys = [k for pc in range(4 * g, 4 * g + 4) for k in qnk(pc)]
                for ki, kb in enumerate(kbs):
                    t_ = tmp[2 * par + ki]
                    tk = ("tmp", 2 * par + ki)
                    if part == 1:
                        P.op(ACT, lambda e, t_=t_, ki=ki, N=N, par=par: e.activation(
                            out=pT2[par][:, ki, 0:N], in_=t_[:, 0:N], func=AF.Exp, scale=0.125),
                            reads=[tk], writes=[PT2K[par][ki]])
                        continue
                    bs = bank(f"g{par}")
                    P.op(PE, lambda e, kb=kb, bs=bs, rhs_q=rhs_q, g=g, lo=lo, hi=hi, N=N, nt=nt: e.matmul(
                        ps[bs][:, 0:N].rearrange("p (a t) -> p a t", t=nt), lhsT=kb["kT"](g)[lo:hi, :],
                        rhs=rhs_q, start=True, stop=True), reads=kb["keys"] + qkeys, writes=pk(bs))
                    t_ = tmp[2 * par + ki]
                    tk = ("tmp", 2 * par + ki)
                    for pr in range(4):
                        head = 8 * g + 2 * pr + par
                        sl = -8.0 * alibi_slope(head)
                        P.op(DVE, lambda e, t_=t_, pr=pr, kb=kb, bs=bs, sl=sl, nt=nt: e.scalar_tensor_tensor(
                            out=t_[:, pr * nt:(pr + 1) * nt], in0=kb["Dm"], scalar=sl,
                            in1=ps[bs][:, pr * nt:(pr + 1) * nt], op0=ALU.mult, op1=ALU.add),
                            reads=pk(bs) + ["D0", "D1", "D1s"], writes=[tk])

            s2ctx = {}

            def stage2(it, part):
                gr, g, par = it
                c0, nt, kbs = gr["c0"], gr["nt"], gr["kbs"]
                N = 4 * nt
                nkb = len(kbs)
                lo, hi = 64 * par, 64 * par + 64
                t2_ = den_buf[par]
                dk_ = den_key[par]
                if part == 1:
                    bo_ = s2ctx[par]
                    P.op(DVE, lambda e, t2_=t2_, bo_=bo_, g=g, lo=lo, hi=hi, N=N, nt=nt, c0=c0: e.tensor_tensor(
                        out=attn_o[lo:hi, 4 * g:4 * g + 4, c0:c0 + nt],
                        in0=ps[bo_][lo:hi, 0:N].rearrange("p (a t) -> p a t", t=nt),
                        in1=t2_[lo:hi, 0:N].rearrange("p (a t) -> p a t", t=nt), op=ALU.mult),
                        reads=pk(bo_) + [dk_], writes=[("xn", pc) for pc in range(4 * g, 4 * g + 4)])
                    return
                bd_ = bank("aux")
                bo_ = bank("mm01")
                s2ctx[par] = bo_
                for ki, kb in enumerate(kbs):
                    P.op(PE, lambda e, ki=ki, bd_=bd_, N=N, nkb=nkb, par=par: e.matmul(
                        ps[bd_][:, 0:N], lhsT=ones_bf[:], rhs=pT2[par][:, ki, 0:N], start=(ki == 0),
                        stop=(ki == nkb - 1)), reads=[PT2K[par][ki], "ones"], writes=pk(bd_))
                for ki, kb in enumerate(kbs):
                    P.op(PE, lambda e, ki=ki, kb=kb, bo_=bo_, g=g, N=N, nkb=nkb, par=par: e.matmul(
                        ps[bo_][:, 0:N], lhsT=kb["v"](g), rhs=pT2[par][:, ki, 0:N], start=(ki == 0),
                        stop=(ki == nkb - 1)), reads=[PT2K[par][ki]] + kb["keys"], writes=pk(bo_))
                P.op(DVE, lambda e, t2_=t2_, bd_=bd_, g=g, par=par, N=N, nt=nt: e.tensor_tensor(
                    out=t2_[:, 0:N].rearrange("p (a t) -> p a t", t=nt),
                    in0=ps[bd_][:, 0:N].rearrange("p (a t) -> p a t", t=nt),
                    in1=esink[:, 8 * g + par:8 * g + 8:2].unsqueeze(2).to_broadcast([128, 4, nt]),
                    op=ALU.add), reads=pk(bd_) + ["esink"], writes=[dk_])
                P.op(ACT, lambda e, t2_=t2_, N=N: e.activation(out=t2_[:, 0:N], in_=t2_[:, 0:N], func=AF.Ln),
                     reads=[dk_], writes=[dk_])
                P.op(ACT, lambda e, t2_=t2_, N=N: e.activation(out=t2_[:, 0:N], in_=t2_[:, 0:N], func=AF.Exp,
                                                               scale=-1.0), reads=[dk_], writes=[dk_])

            NI = len(items)
            assert all(items[i][2] == i % 2 for i in range(NI))
            stage1(items[0], 0)
            if NI > 1:
                stage1(items[1], 0)
            stage1(items[0], 1)
            for n in range(NI):
                stage2(items[n], 0)
                if n + 2 < NI:
                    stage1(items[n + 2], 0)
                stage2(items[n], 1)
                if n + 1 < NI:
                    stage1(items[n + 1], 1)
            if sstop == 12:
                return
            for s in range(4):
                (wo_,), ko_ = wload([(b_wo_v[:, :, 512 * s:512 * s + 512], KC, 512)])
                for o4 in range(4):
                    oc = 4 * s + o4
                    bk = proj_fm(wo_, ko_, 128 * o4, lambda k: attn_o[:, k, 0:T], xk, KC, T)
                    resid_add(oc, bk, T)

        stop_after = (dbg or {}).get("stop_after")
        first_p = True
        for ti, tile in enumerate(tiles):
            T = 512 if tile["kind"] == "p" else NSEQ_S * LS
            wtile[0], wtile[1] = 0, ti
            load_x(tile, T)
            phases = [lambda: hgrn(tile, T), lambda: ffn(0, T), lambda: ple(0, tile, T),
                      lambda: swa(tile, T, first_p), lambda: ffn(1, T), lambda: ple(1, tile, T)]
            for i, ph in enumerate(phases):
                if stop_after is not None and i > stop_after:
                    break
                ph()
            if tile["kind"] == "p":
                first_p = False
            store_y(tile, T)
        with nc.allow_non_contiguous_dma(reason="tiny per-feature vectors / strided cache rows"):
            P.emit()
    return nc


TILES_FULL = [dict(kind="p", t0=512 * i) for i in range(4)] + [dict(kind="s")]
N_CORES = 8


def make_in_maps(inp, cores):
    f = lambda a: np.ascontiguousarray(np.asarray(a, dtype=np.float32))
    shared = dict(
        nmix=f(inp["norm_mix"]), nffn=f(inp["norm_ffn"]), nple=f(inp["norm_ple"]),
        w_in=f(inp["a_w_in"][0]), lbl=f(inp["a_lb_logits"]), gnorm=f(inp["a_g_norm"]),
        a_wo=f(inp["a_w_o"][0]), wqkv=f(inp["b_w_qkv"][0]), qnorm=f(inp["b_q_norm"]),
        knorm=f(inp["b_k_norm"]), sinks=f(inp["b_sinks"]), b_wo=f(inp["b_w_o"][0]),
        wgu=f(inp["f_w_gu"]), wdown=f(inp["f_w_down"]), wpp=f(inp["ple_w_proj"]), wpg=f(inp["ple_w_gate"]))
    maps = []
    for c in cores:
        s0, s1 = NSEQ_S * c, NSEQ_S * c + NSEQ_S
        m = dict(shared)
        m.update(
            xp=f(inp["x_prompt"][c]),
            xs=f(inp["x_sample"][s0:s1]).reshape(NSEQ_S * LS, D),
            st=f(inp["state_hgrn"][0, s0:s1]),
            ck=f(inp["cache_k"][0, s0:s1]).reshape(NSEQ_S, 128, 256),
            cv=f(inp["cache_v"][0, s0:s1]).reshape(NSEQ_S, 128, 256),
            pp=f(inp["p_prompt"][:, c]),
            psm=f(inp["p_sample"][:, s0:s1]).reshape(2, NSEQ_S * LS, 256))
        maps.append(m)
    return maps


def kernel(**inp):
    nc = build(TILES_FULL)
    cores = list(range(N_CORES))
    res = run_bass_kernel_spmd(nc, make_in_maps(inp, cores), core_ids=cores)
    r = res.results
    y_prompt = np.stack([r[c]["yp"] for c in cores]).astype(np.float32)
    y_sample = np.concatenate([r[c]["ys"].reshape(NSEQ_S, LS, D) for c in cores]).astype(np.float32)
    st_p = np.stack([r[c]["sp"] for c in cores])[None].astype(np.float32)
    st_s = np.concatenate([r[c]["ss"] for c in cores])[None].astype(np.float32)
    ck_p = np.stack([r[c]["ckp"].reshape(128, 4, 64) for c in cores])[None].astype(np.float32)
    cv_p = np.stack([r[c]["cvp"].reshape(128, 4, 64) for c in cores])[None].astype(np.float32)
    ck_s = np.concatenate([r[c]["cks"].reshape(NSEQ_S, LS, 4, 64) for c in cores])[None].astype(np.float32)
    cv_s = np.concatenate([r[c]["cvs"].reshape(NSEQ_S, LS, 4, 64) for c in cores])[None].astype(np.float32)
    return (y_prompt, y_sample, st_p, st_s, ck_p, cv_p, ck_s, cv_s)
```

```python
import contextlib
import numpy as np
import concourse.bass as bass
import concourse.mybir as mybir
from concourse.bass_utils import run_bass_kernel_spmd

F32 = mybir.dt.float32
BF16 = mybir.dt.bfloat16
AF = mybir.ActivationFunctionType
ALU = mybir.AluOpType
AX = mybir.AxisListType

PE, ACT, DVE, POOL, SP = "pe", "act", "dve", "pool", "sp"
ENGINES = (PE, ACT, DVE, POOL, SP)

D = 2048
KC = 16
FF = 5632
FC = 44
SEQ = 2048
NSEQ_S = 4
LS = 64
EPS = 1e-6
TMAX = 512
NSLOT = 3
DMASK = 1.0e5


class Op:
    __slots__ = ("eng", "seq", "fn", "waits", "needs_inc", "is_dma", "sem_key", "dma_idx", "tick")

    def __init__(self, eng, seq, fn, is_dma=False, sem_key=None):
        self.eng = eng
        self.seq = seq
        self.fn = fn
        self.waits = []
        self.needs_inc = False
        self.is_dma = is_dma
        self.sem_key = sem_key
        self.dma_idx = 0
        self.tick = 0


class Prog:
    def __init__(self, nc):
        self.nc = nc
        self.q = {e: [] for e in ENGINES}
        self.last_w = {}
        self.readers = {}
        self.seen = {c: {p: -1 for p in ENGINES} for c in ENGINES}
        self.dma_seen = {c: set() for c in ENGINES}
        self.dma_count = {}
        self.dma_last = {}
        self.out_dmas = []

    def _add_dep(self, op, dep):
        if dep is None or dep is op:
            return
        if dep.is_dma:
            if id(dep) in self.dma_seen[op.eng]:
                return
            self.dma_seen[op.eng].add(id(dep))
            op.waits.append(dep)
            return
        if dep.eng == PE and op.eng == PE:
            return
        if dep.seq <= self.seen[op.eng][dep.eng]:
            return
        self.seen[op.eng][dep.eng] = dep.seq
        dep.needs_inc = True
        op.waits.append(dep)

    def _track(self, op, reads, writes):
        best = {}
        dmas = []

        def cand(d):
            if d is None or d is op:
                return
            if d.is_dma:
                dmas.append(d)
            elif d.eng not in best or d.seq > best[d.eng].seq:
                best[d.eng] = d

        for k in reads:
            cand(self.last_w.get(k))
        for k in writes:
            cand(self.last_w.get(k))
            for r in self.readers.get(k, ()):
                cand(r)
        for d in dmas:
            self._add_dep(op, d)
        for d in best.values():
            self._add_dep(op, d)
        for k in reads:
            self.readers.setdefault(k, []).append(op)
        for k in writes:
            self.last_w[k] = op
            self.readers[k] = []

    def op(self, eng, fn, reads=(), writes=()):
        o = Op(eng, len(self.q[eng]), fn)
        self._track(o, reads, writes)
        self.q[eng].append(o)
        return o

    def dma(self, eng, fn, reads=(), writes=(), sem_key=None, is_output=False):
        o = Op(eng, len(self.q[eng]), fn, is_dma=True, sem_key=sem_key)
        prev = self.dma_last.get(sem_key)
        if prev is not None:
            self._add_dep(o, prev)
        self._track(o, reads, writes)
        o.dma_idx = self.dma_count.get(sem_key, 0)
        self.dma_count[sem_key] = o.dma_idx + 1
        self.dma_last[sem_key] = o
        self.q[eng].append(o)
        if is_output:
            self.out_dmas.append(o)
        return o

    def emit(self):
        nc = self.nc
        fin = Op(SP, len(self.q[SP]), None)
        lastout = {}
        for o in self.out_dmas:
            lastout[o.sem_key] = o
        for o in lastout.values():
            fin.waits.append(o)
        self.q[SP].append(fin)
        EPOCH = 1500
        nep = {}
        for e in ENGINES:
            t = 0
            for o in self.q[e]:
                if o.needs_inc:
                    t += 1
                o.tick = t
            nep[e] = t // EPOCH + 1
        sem_names = [(e, k) for e in ENGINES for k in range(nep[e])] + [("dma", k) for k in self.dma_count]
        with contextlib.ExitStack() as st:
            sems = {}
            for i, n in enumerate(sem_names):
                sems[n] = st.enter_context(nc.semaphore(f"s{i}"))
            block = st.enter_context(nc.Block())

            def run(engname, eng):
                for o in self.q[engname]:
                    for d in o.waits:
                        if d.is_dma:
                            eng.wait_ge(sems[("dma", d.sem_key)], 16 * (d.dma_idx + 1))
                        else:
                            eng.wait_ge(sems[(d.eng, (d.tick - 1) // EPOCH)], (d.tick - 1) % EPOCH + 1)
                    if o.fn is None:
                        continue
                    ins = o.fn(eng)
                    if o.is_dma:
                        ins.then_inc(sems[("dma", o.sem_key)], 16)
                    elif o.needs_inc:
                        ins.then_inc(sems[(engname, (o.tick - 1) // EPOCH)], 1)

            @block.tensor
            def _(eng):
                run(PE, eng)

            @block.scalar
            def _(eng):
                run(ACT, eng)

            @block.vector
            def _(eng):
                run(DVE, eng)

            @block.gpsimd
            def _(eng):
                run(POOL, eng)

            @block.sync
            def _(eng):
                run(SP, eng)


def alibi_slope(h):
    return float(2.0 ** (-8.0 * (h + 1) / 32.0))


def build(tiles, dbg=None):
    nc = bass.Bass("TRN2", target_bir_lowering=False)

    def din(name, shape):
        return nc.dram_tensor(name, list(shape), F32, kind="ExternalInput").ap()

    def dout(name, shape):
        return nc.dram_tensor(name, list(shape), F32, kind="ExternalOutput").ap()

    xp = din("xp", [SEQ, D])
    xs = din("xs", [NSEQ_S * LS, D])
    st_in = din("st", [NSEQ_S, 16, 128, 128])
    ck_in = din("ck", [NSEQ_S, 128, 256])
    cv_in = din("cv", [NSEQ_S, 128, 256])
    pp = din("pp", [2, SEQ, 256])
    psm = din("psm", [2, NSEQ_S * LS, 256])
    nmix = din("nmix", [2, D])
    nffn = din("nffn", [2, D])
    nple = din("nple", [2, D])
    w_in = din("w_in", [D, 8192])
    lbl = din("lbl", [3, D])
    gnorm = din("gnorm", [1, D])
    a_wo = din("a_wo", [D, D])
    wqkv = din("wqkv", [D, 2560])
    qnorm = din("qnorm", [1, 64])
    knorm = din("knorm", [1, 64])
    sinks = din("sinks", [1, 32])
    b_wo = din("b_wo", [D, D])
    wgu = din("wgu", [2, D, 2 * FF])
    wdown = din("wdown", [2, FF, D])
    wpp = din("wpp", [2, 256, D])
    wpg = din("wpg", [2, D, D])

    yp = dout("yp", [SEQ, D])
    ys = dout("ys", [NSEQ_S * LS, D])
    sp_out = dout("sp", [16, 128, 128])
    ss_out = dout("ss", [NSEQ_S, 16, 128, 128])
    ckp = dout("ckp", [128, 256])
    cvp = dout("cvp", [128, 256])
    cks = dout("cks", [NSEQ_S * LS, 256])
    cvs = dout("cvs", [NSEQ_S * LS, 256])

    NSTR = 121
    wscr = nc.dram_tensor("wscr", [NSTR, 128, 8192], BF16, kind="Internal").ap()

    P = Prog(nc)
    st = contextlib.ExitStack()
    with st:
        def sb(name, shape, dt):
            return st.enter_context(nc.sbuf_tensor(name, shape, dt))

        h = sb("h", [128, KC, TMAX], F32)
        xn = sb("xn", [128, KC, TMAX], BF16)
        R = sb("R", [128, 22528], BF16)
        Rf = R.bitcast(F32)
        Q = sb("Q", [128, 12288], BF16)
        Qf = Q.bitcast(F32)
        W = [sb(f"W{i}", [128, 8192], BF16) for i in range(NSLOT)]
        sstate = sb("sstate", [128, 16, 128], F32)
        sbf = sb("sbf", [128, 2, 8, 128], BF16)
        tmp = [sb(f"tmp{i}", [128, TMAX], F32) for i in range(8)]
        sx = sb("sx", [128, 2, 128], F32)
        sq = [sb(f"sq{i}", [128, TMAX], BF16) for i in range(2)]
        rstd = sb("rstd", [128, TMAX], F32)
        pT = sb("pT", [128, 2, TMAX], BF16)
        egl = sb("egl", [128, 8, 8], F32)
        ident = sb("ident", [128, 128], F32)
        ones_bf = sb("ones_bf", [128, 128], BF16)
        bd_bf = sb("bd_bf", [128, 128], BF16)
        scanmask = sb("scanmask", [128, TMAX], BF16)
        bdmask = sb("bdmask", [128, 128], F32)
        D0 = sb("D0", [128, 128], F32)
        D1 = sb("D1", [128, 128], F32)
        D1s = sb("D1s", [128, 128], F32)
        gl = Rf[0:16, 0:1280].rearrange("p (a b) -> p a b", b=128)
        gcols = sb("gcols", [128, 10, 16], F32)
        lbt = sb("lbt", [128, 8, 16], F32)
        qg = sb("qg", [128, 1], F32)
        kg_bc = sb("kg_bc", [128, 64], F32)
        esink = sb("esink", [128, 32], F32)
        ksm = sb("ksm", [128, 2, 4], F32)
        kcar = sb("kcar", [128, 4, 128], BF16)
        vcar = sb("vcar", [128, 4, 128], BF16)

        ps = [st.enter_context(nc.psum_tensor(f"ps{i}", [128, 512], F32)) for i in range(8)]
        pools = {"mm": [0, 1, 2, 3], "aux": [4, 5], "g": [6, 7], "g0": [6, 2], "g1": [7, 3], "mm01": [0, 1]}
        pool_ctr = {k: 0 for k in pools}

        def bank(pool):
            i = pools[pool][pool_ctr[pool] % len(pools[pool])]
            pool_ctr[pool] += 1
            return i

        def pk(i):
            return [("ps", i)]

        def rk(region, lo, hi):
            return [(region, p) for p in range(lo // 512, (hi + 511) // 512)]

        wctr = [0]
        wtile = [0, 0]

        def wload(parts):
            s = wctr[0] % NSLOT
            wctr[0] += 1
            idx = wtile[0]
            wtile[0] += 1
            assert idx < NSTR
            off = 0
            views, keys = [], [("Wslot", s)]
            for pi, (view, kc, n) in enumerate(parts):
                views.append(W[s][:, off:off + kc * n].rearrange("p (k n) -> p k n", n=n))
                keys.append(("W", s, pi))
                off += kc * n
            assert off <= 8192
            ti_ = wtile[1]
            cast = ti_ == 0 or (ti_ == 1 and idx % 2 == 1)
            wback = (ti_ == 0 and idx % 2 == 0) or (ti_ == 1 and idx % 2 == 1)
            if cast:
                for pi, (view, kc, n) in enumerate(parts):
                    wr = [("W", s, pi)] + ([("Wslot", s)] if pi == 0 else [])
                    P.dma(POOL, lambda e, dst=views[pi], view=view: e.dma_start(out=dst, in_=view),
                          writes=wr, sem_key=("W", s, pi))
                if wback:
                    P.dma(SP, lambda e, idx=idx, s=s, off=off: e.dma_start(out=wscr[idx][:, 0:off], in_=W[s][:, 0:off]),
                          reads=keys, writes=[("scr", idx)], sem_key=("wb", s))
            else:
                P.dma(POOL, lambda e, idx=idx, s=s, off=off: e.dma_start(out=W[s][:, 0:off], in_=wscr[idx][:, 0:off]),
                      reads=[("scr", idx)], writes=keys, sem_key=("W", s, 0))
            return views, keys

        w_in_v = w_in.rearrange("(kc p) n -> p kc n", p=128)
        a_wo_v = a_wo.rearrange("(kc p) n -> p kc n", p=128)
        wqkv_v = wqkv.rearrange("(kc p) n -> p kc n", p=128)
        b_wo_v = b_wo.rearrange("(kc p) n -> p kc n", p=128)
        wgu_v = [wgu[l].rearrange("(kc p) n -> p kc n", p=128) for l in range(2)]
        wdown_v = [wdown[l].rearrange("(kc p) n -> p kc n", p=128) for l in range(2)]
        wpp_v = [wpp[l].rearrange("(kc p) n -> p kc n", p=128) for l in range(2)]
        wpg_v = [wpg[l].rearrange("(kc p) n -> p kc n", p=128) for l in range(2)]

        P.op(POOL, lambda e: e.memset(sstate[:], 0.0), writes=[("S", i) for i in range(16)])
        P.op(POOL, lambda e: e.memset(ident[:], 0.0), writes=["ident"])
        P.op(POOL, lambda e: e.affine_select(out=ident[:], in_=ident[:], pattern=[[-1, 128]], base=0,
                                             channel_multiplier=1, compare_op=ALU.not_equal, fill=1.0),
             reads=["ident"], writes=["ident"])
        P.op(POOL, lambda e: e.memset(ones_bf[:], 1.0), writes=["ones"])
        P.op(POOL, lambda e: e.memset(bd_bf[:], 0.0), writes=["bd"])
        P.op(POOL, lambda e: e.memset(bd_bf[0:64, 0:64], 1.0), writes=["bd"])
        P.op(POOL, lambda e: e.memset(bd_bf[64:128, 64:128], 1.0), writes=["bd"])
        P.op(POOL, lambda e: e.memset(scanmask[:], 1.0), writes=["scanmask"])
        P.op(POOL, lambda e: e.memset(scanmask[:].rearrange("p (c t) -> p c t", t=64)[:, :, 0:1], 0.0),
             writes=["scanmask"])
        P.op(POOL, lambda e: e.memset(bdmask[:], 1.0), writes=["bdmask"])
        P.op(POOL, lambda e: e.affine_select(out=bdmask[:], in_=bdmask[:], pattern=[[1, 128]], base=0,
                                             channel_multiplier=-1, compare_op=ALU.is_ge, fill=0.0),
             reads=["bdmask"], writes=["bdmask"])
        P.op(POOL, lambda e: e.memset(bdmask[0:64, 64:128], 0.0), writes=["bdmask"])
        P.op(POOL, lambda e: e.iota(D0[:], pattern=[[1, 128]], base=128, channel_multiplier=-1,
                                    allow_small_or_imprecise_dtypes=True), writes=["D0"])
        P.op(POOL, lambda e: e.memset(D0[0:64, 64:128], DMASK), writes=["D0"])
        P.op(POOL, lambda e: e.iota(D1[:], pattern=[[1, 128]], base=0, channel_multiplier=-1,
                                    allow_small_or_imprecise_dtypes=True), writes=["D1"])
        P.op(DVE, lambda e: e.tensor_scalar(out=D1s[:], in0=D1[:], scalar1=-1.0, scalar2=None, op0=ALU.mult),
             reads=["D1"], writes=["D1s"])
        P.op(DVE, lambda e: e.tensor_tensor(out=D1[:], in0=D1[:], in1=D1s[:], op=ALU.max),
             reads=["D1", "D1s"], writes=["D1"])
        P.op(POOL, lambda e: e.memset(D1[64:128, 0:64], DMASK), writes=["D1"])
        P.op(POOL, lambda e: e.tensor_copy(out=D1s[:], in_=D1[:]), reads=["D1"], writes=["D1s"])
        P.op(POOL, lambda e: e.memset(D1s[0:64, 64:128], DMASK), writes=["D1s"])
        vecs = [nmix[0], nmix[1], nffn[0], nffn[1], nple[0], nple[1], gnorm[0], lbl[0], lbl[1], lbl[2]]
        for i, v in enumerate(vecs):
            P.dma(SP, lambda e, i=i, v=v: e.dma_start(out=gl[:, i, :], in_=v.rearrange("(k p) -> k p", p=128)),
                  writes=[("R", 0)], sem_key=("gl", i))
        bg = bank("aux")
        for i in range(10):
            P.op(PE, lambda e, i=i: e.transpose(out=ps[bg][:, 16 * i:16 * i + 16], in_=gl[:, i, :],
                                                identity=ident[0:16, 0:16]),
                 reads=[("R", 0), "ident"], writes=pk(bg))
        P.op(DVE, lambda e: e.tensor_copy(out=gcols[:].rearrange("p a b -> p (a b)"), in_=ps[bg][:, 0:160]),
             reads=pk(bg), writes=["gcols"])
        P.op(ACT, lambda e: e.activation(out=lbt[:, 0:3, :], in_=gcols[:, 7:10, :], func=AF.Exp),
             reads=["gcols"], writes=["lbt"])
        P.op(DVE, lambda e: e.tensor_tensor(out=lbt[:, 3, :], in0=lbt[:, 0, :], in1=lbt[:, 1, :], op=ALU.add),
             reads=["lbt"], writes=["lbt"])
        P.op(DVE, lambda e: e.tensor_tensor(out=lbt[:, 3, :], in0=lbt[:, 3, :], in1=lbt[:, 2, :], op=ALU.add),
             reads=["lbt"], writes=["lbt"])
        P.op(DVE, lambda e: e.reciprocal(out=lbt[:, 4, :], in_=lbt[:, 3, :]), reads=["lbt"], writes=["lbt"])
        P.op(DVE, lambda e: e.tensor_tensor(out=lbt[:, 5, :], in0=lbt[:, 0, :], in1=lbt[:, 4, :], op=ALU.mult),
             reads=["lbt"], writes=["lbt"])
        P.op(DVE, lambda e: e.tensor_scalar(out=lbt[:, 6, :], in0=lbt[:, 5, :], scalar1=-1.0, scalar2=1.0,
                                            op0=ALU.mult, op1=ALU.add), reads=["lbt"], writes=["lbt"])
        P.op(ACT, lambda e: e.activation(out=lbt[:, 7, :], in_=lbt[:, 6, :], func=AF.Ln),
             reads=["lbt"], writes=["lbt"])
        for hf in range(2):
            P.dma(SP, lambda e, hf=hf: e.dma_start(out=qg[64 * hf:64 * hf + 64, :],
                                                   in_=qnorm.rearrange("o d -> d o")),
                  writes=["qg"], sem_key=("qg", hf))
        P.dma(SP, lambda e: e.dma_start(out=kg_bc[:], in_=knorm[0].partition_broadcast(128)),
              writes=["kg"], sem_key="kg")
        P.dma(SP, lambda e: e.dma_start(out=esink[:], in_=sinks[0].partition_broadcast(128)),
              writes=["esink"], sem_key="esink")
        P.op(ACT, lambda e: e.activation(out=esink[:], in_=esink[:], func=AF.Exp),
             reads=["esink"], writes=["esink"])

        evac_ctr = [0]

        def evac(out, in_, reads, writes):
            evac_ctr[0] += 1
            if evac_ctr[0] % 2:
                P.op(ACT, lambda e: e.activation(out=out, in_=in_, func=AF.Copy), reads=reads, writes=writes)
            else:
                P.op(DVE, lambda e: e.tensor_copy(out=out, in_=in_), reads=reads, writes=writes)

        def hk(kc):
            return [("h", kc)]

        def xk(kc):
            return [("xn", kc)]

        ALLX = [("xn", kc) for kc in range(KC)]

        def xstg(b):
            if b < 3:
                return Qf[:, 2048 * b:2048 * b + 2048], rk("Q", 4096 * b, 4096 * b + 4096)
            return Rf[:, 8192:10240], rk("R", 16384, 20480)

        def issue_x(tile, T):
            NB = T // 128
            for b in range(NB):
                src = xp[tile["t0"] + 128 * b: tile["t0"] + 128 * b + 128, :] if tile["kind"] == "p" \
                    else xs[128 * b:128 * b + 128, :]
                dst, keys = xstg(b)
                P.dma(SP, lambda e, dst=dst, src=src: e.dma_start(out=dst, in_=src), writes=keys, sem_key=("xin", b))

        def unpack_x(tile, T):
            NB = T // 128
            for kc in range(KC):
                bk = bank("aux")
                for b in range(NB):
                    stg, keys = xstg(b)
                    P.op(PE, lambda e, b=b, kc=kc, bk=bk, stg=stg: e.transpose(
                        out=ps[bk][:, 128 * b:128 * b + 128], in_=stg[:, 128 * kc:128 * kc + 128], identity=ident[:]),
                        reads=keys + ["ident"], writes=pk(bk))
                evac(h[:, kc, 0:T], ps[bk][:, 0:T], pk(bk), hk(kc))

        def store_y(tile, T):
            NB = T // 128
            for b in range(NB):
                for g4 in range(4):
                    bk = bank("aux")
                    for k4 in range(4):
                        kc = 4 * g4 + k4
                        P.op(PE, lambda e, b=b, kc=kc, k4=k4, bk=bk: e.transpose(
                            out=ps[bk][:, 128 * k4:128 * k4 + 128], in_=h[:, kc, 128 * b:128 * b + 128],
                            identity=ident[:]), reads=hk(kc) + ["ident"], writes=pk(bk))
                    lo = 2048 * b + 512 * g4
                    evac(Rf[:, lo:lo + 512], ps[bk][:, :], pk(bk), rk("R", 2 * lo, 2 * lo + 1024))
                dst = yp[tile["t0"] + 128 * b: tile["t0"] + 128 * b + 128, :] if tile["kind"] == "p" \
                    else ys[128 * b:128 * b + 128, :]
                P.dma(SP, lambda e, b=b, dst=dst: e.dma_start(out=dst, in_=Rf[:, 2048 * b:2048 * b + 2048]),
                      reads=rk("R", 4096 * b, 4096 * b + 4096), sem_key=("yout", b), is_output=True)

        def rms_stats(src_fn, src_keys_fn, T, inv_n):
            bk = bank("aux")
            for kc in range(KC):
                s = sq[kc % 2]
                P.op(ACT, lambda e, kc=kc, s=s: e.activation(out=s[:, 0:T], in_=src_fn(kc), func=AF.Square),
                     reads=src_keys_fn(kc), writes=[("sq", kc % 2)])
                P.op(PE, lambda e, kc=kc, s=s, bk=bk: e.matmul(ps[bk][:, 0:T], lhsT=ones_bf[:], rhs=s[:, 0:T],
                                                              start=(kc == 0), stop=(kc == KC - 1)),
                     reads=[("sq", kc % 2), "ones"], writes=pk(bk))
            P.op(ACT, lambda e, bk=bk: e.activation(out=rstd[:, 0:T], in_=ps[bk][:, 0:T], func=AF.Ln,
                                                    scale=inv_n, bias=EPS), reads=pk(bk), writes=["rstd"])
            P.op(ACT, lambda e: e.activation(out=rstd[:, 0:T], in_=rstd[:, 0:T], func=AF.Exp, scale=-0.5),
                 reads=["rstd"], writes=["rstd"])

        def rmsnorm_h(gi, T):
            rms_stats(lambda kc: h[:, kc, 0:T], hk, T, 1.0 / D)
            for kc in range(KC):
                P.op(DVE, lambda e, kc=kc: e.scalar_tensor_tensor(
                    out=xn[:, kc, 0:T], in0=h[:, kc, 0:T], scalar=gcols[:, gi, kc:kc + 1], in1=rstd[:, 0:T],
                    op0=ALU.mult, op1=ALU.mult), reads=hk(kc) + ["gcols", "rstd"], writes=xk(kc))

        def proj_fm(wv, wkeys, col0, rhs_fn, rhs_keys, nk, T, pool="mm"):
            bk = bank(pool)
            for k in range(nk):
                P.op(PE, lambda e, k=k, bk=bk: e.matmul(ps[bk][:, 0:T], lhsT=wv[:, k, col0:col0 + 128],
                                                       rhs=rhs_fn(k), start=(k == 0), stop=(k == nk - 1)),
                     reads=wkeys + rhs_keys(k), writes=pk(bk))
            return bk

        def resid_add(oc, bk, T):
            P.op(DVE, lambda e: e.tensor_tensor(out=h[:, oc, 0:T], in0=ps[bk][:, 0:T], in1=h[:, oc, 0:T],
                                                op=ALU.add), reads=pk(bk) + hk(oc), writes=hk(oc))

        def ffn(l, T):
            rmsnorm_h(2 + l, T)
            act = R[:, 0:FC * TMAX].rearrange("p (j t) -> p j t", t=TMAX)
            for s in range(22):
                (wg_, wu_), kgu_ = wload([(wgu_v[l][:, :, 256 * s:256 * s + 256], KC, 256),
                                          (wgu_v[l][:, :, FF + 256 * s:FF + 256 * s + 256], KC, 256)])
                for j4 in range(2):
                    j = 2 * s + j4
                    bg_ = proj_fm(wg_, kgu_, 128 * j4, lambda k: xn[:, k, 0:T], xk, KC, T)
                    bu_ = proj_fm(wu_, kgu_, 128 * j4, lambda k: xn[:, k, 0:T], xk, KC, T)
                    t_ = tmp[j % 2]
                    P.op(ACT, lambda e, t_=t_, bg_=bg_: e.activation(out=t_[:, 0:T], in_=ps[bg_][:, 0:T],
                                                                     func=AF.Silu),
                         reads=pk(bg_), writes=[("tmp", j % 2)])
                    P.op(DVE, lambda e, t_=t_, bu_=bu_, j=j: e.tensor_tensor(
                        out=act[:, j, 0:T], in0=ps[bu_][:, 0:T], in1=t_[:, 0:T], op=ALU.mult),
                        reads=pk(bu_) + [("tmp", j % 2)], writes=rk("R", j * TMAX, j * TMAX + TMAX))
            QK = FC // 4
            for og in range(KC // 4):
                bks = [bank("mm") for _ in range(4)]
                for qt in range(4):
                    (wd_,), kd_ = wload([(wdown_v[l][:, QK * qt:QK * qt + QK, 512 * og:512 * og + 512], QK, 512)])
                    for o4 in range(4):
                        for k in range(QK):
                            kk = QK * qt + k
                            P.op(PE, lambda e, k=k, kk=kk, o4=o4, qt=qt, wd_=wd_, bk=bks[o4]: e.matmul(
                                ps[bk][:, 0:T], lhsT=wd_[:, k, 128 * o4:128 * o4 + 128], rhs=act[:, kk, 0:T],
                                start=(qt == 0 and k == 0), stop=(qt == 3 and k == QK - 1)),
                                reads=kd_ + rk("R", kk * TMAX, kk * TMAX + TMAX), writes=pk(bks[o4]))
                for o4 in range(4):
                    resid_add(4 * og + o4, bks[o4], T)

        def ple(l, tile, T):
            NB = T // 128
            rmsnorm_h(4 + l, T)
            for b in range(NB):
                src = pp[l, tile["t0"] + 128 * b: tile["t0"] + 128 * b + 128, :] if tile["kind"] == "p" \
                    else psm[l, 128 * b:128 * b + 128, :]
                P.dma(SP, lambda e, b=b, src=src: e.dma_start(out=Rf[:, 10240 + 256 * b:10240 + 256 * b + 256], in_=src),
                      writes=rk("R", 20480 + 512 * b, 20480 + 512 * b + 512), sem_key=("pin", b))
            for k2 in range(2):
                bk = bank("aux")
                for b in range(NB):
                    P.op(PE, lambda e, b=b, k2=k2, bk=bk: e.transpose(
                        out=ps[bk][:, 128 * b:128 * b + 128],
                        in_=Rf[:, 10240 + 256 * b + 128 * k2:10240 + 256 * b + 128 * k2 + 128], identity=ident[:]),
                        reads=rk("R", 20480 + 512 * b, 20480 + 512 * b + 512) + ["ident"], writes=pk(bk))
                evac(pT[:, k2, 0:T], ps[bk][:, 0:T], pk(bk), [("pT", k2)])
            for s in range(8):
                (wg_, wpp_), kg_ = wload([(wpg_v[l][:, :, 256 * s:256 * s + 256], KC, 256),
                                          (wpp_v[l][:, :, 256 * s:256 * s + 256], 2, 256)])
                for o4 in range(2):
                    oc = 2 * s + o4
                    bg_ = proj_fm(wg_, kg_, 128 * o4, lambda k: xn[:, k, 0:T], xk, KC, T)
                    bp_ = proj_fm(wpp_, kg_, 128 * o4, lambda k: pT[:, k, 0:T], lambda k: [("pT", k)], 2, T)
                    t_ = tmp[oc % 2]
                    P.op(ACT, lambda e, t_=t_, bg_=bg_: e.activation(out=t_[:, 0:T], in_=ps[bg_][:, 0:T],
                                                                     func=AF.Sigmoid),
                         reads=pk(bg_), writes=[("tmp", oc % 2)])
                    P.op(DVE, lambda e, t_=t_, bp_=bp_: e.tensor_tensor(
                        out=t_[:, 0:T], in0=ps[bp_][:, 0:T], in1=t_[:, 0:T], op=ALU.mult),
                        reads=pk(bp_) + [("tmp", oc % 2)], writes=[("tmp", oc % 2)])
                    P.op(DVE, lambda e, t_=t_, oc=oc: e.tensor_tensor(
                        out=h[:, oc, 0:T], in0=h[:, oc, 0:T], in1=t_[:, 0:T], op=ALU.add),
                        reads=hk(oc) + [("tmp", oc % 2)], writes=hk(oc))

        def hgrn(tile, T):
            NB = T // 128
            NCH = T // 64
            is_s = tile["kind"] == "s"
            rmsnorm_h(0, T)
            hstop = (dbg or {}).get("hstop", 99)
            if hstop == 0:
                return
            o_sb = Rf[:, 0:KC * TMAX].rearrange("p (k t) -> p k t", t=TMAX)
            v_sb = R[:, 16384:16384 + 4096].rearrange("p (b f) -> p b f", f=1024)
            q1 = Q[:, 0:4096].rearrange("p (a t) -> p a t", t=TMAX)
            k2 = Q[:, 4096:8192].rearrange("p (a t) -> p a t", t=TMAX)
            k3T = Q[:, 8192:12288].rearrange("p (a b d) -> p a b d", b=4, d=128)

            def okeys(hh):
                return rk("R", 2 * hh * TMAX, 2 * hh * TMAX + 2 * TMAX)

            def q1k(a):
                return rk("Q", a * TMAX, a * TMAX + TMAX)

            def k2k(a):
                return rk("Q", 4096 + a * TMAX, 4096 + a * TMAX + TMAX)

            def k3k(a):
                return rk("Q", 8192 + a * TMAX, 8192 + a * TMAX + TMAX)

            VK = rk("R", 16384, 16384 + 4096)

            lastp = (not is_s) and tile["t0"] + T == SEQ

            for hg in range(2):
                ctx = {}

                def V():
                    for s2 in range(2):
                        c0 = 4096 + (hg * 8 + s2 * 4) * 128
                        (wv_,), kv_ = wload([(w_in_v[:, :, c0:c0 + 512], KC, 512)])
                        for b in range(NB):
                            bk = bank("mm")
                            for k in range(KC):
                                P.op(PE, lambda e, k=k, b=b, bk=bk, wv_=wv_: e.matmul(
                                    ps[bk][:, :], lhsT=xn[:, k, 128 * b:128 * b + 128], rhs=wv_[:, k, :],
                                    start=(k == 0), stop=(k == KC - 1)), reads=kv_ + xk(k), writes=pk(bk))
                            evac(v_sb[:, b, 512 * s2:512 * s2 + 512], ps[bk][:, :], pk(bk), VK)

                def Pj(a):
                    if a % 2 == 0:
                        cq = (hg * 8 + a) * 128
                        ctx["w"] = wload([(w_in_v[:, :, cq:cq + 256], KC, 256),
                                          (w_in_v[:, :, 2048 + cq:2048 + cq + 256], KC, 256)])
                    (wq_, wf_), kq_ = ctx["w"]
                    hh = a % 2
                    bq = proj_fm(wq_, kq_, 128 * hh, lambda k: xn[:, k, 0:T], xk, KC, T)
                    bf = proj_fm(wf_, kq_, 128 * hh, lambda k: xn[:, k, 0:T], xk, KC, T)
                    ctx[a] = (bq, bf)

                def tms(a):
                    p4 = 4 * (a % 2)
                    return [tmp[p4 + i][:, 0:T] for i in range(4)], [("tmp", p4 + i) for i in range(4)]

                def S1(a):
                    bq, bf = ctx[a]
                    hd = hg * 8 + a
                    (t0_, t1_, t2_, t3_), TK = tms(a)
                    qps, fps = ps[bq][:, 0:T], ps[bf][:, 0:T]
                    lbc = lbt[:, 5, hd:hd + 1]
                    P.op(ACT, lambda e: e.activation(out=t0_, in_=qps, func=AF.Exp, scale=-1.0),
                         reads=pk(bq), writes=[TK[0]])
                    P.op(ACT, lambda e: e.activation(out=t0_, in_=t0_, func=AF.Ln, bias=1.0),
                         reads=[TK[0]], writes=[TK[0]])
                    P.op(ACT, lambda e: e.activation(out=t1_, in_=fps, func=AF.Exp, scale=-1.0),
                         reads=pk(bf), writes=[TK[1]])
                    P.op(ACT, lambda e: e.activation(out=t2_, in_=t1_, func=AF.Ln, scale=lbc, bias=1.0),
                         reads=[TK[1], "lbt"], writes=[TK[2]])
                    P.op(ACT, lambda e: e.activation(out=t1_, in_=t1_, func=AF.Ln, bias=1.0),
                         reads=[TK[1]], writes=[TK[1]])

                def chain_steps(a, c0_, c1_):
                    hd = hg * 8 + a
                    par = a % 2
                    ub = ctx[("ub", a)]
                    Sb = [sstate[:, hd, :], sx[:, par, :]]
                    SbK = [("S", hd), ("sx", par)]
                    for c in range(c0_, c1_):
                        U = ps[ub[c % 2]][:, 128 * (c // 2):128 * (c // 2) + 128]
                        dcol = egl[:, a, c:c + 1]
                        src, dst = Sb[c % 2], Sb[(c + 1) % 2]
                        P.op(ACT, lambda e, src=src, c=c: e.activation(out=sbf[:, par, c, :], in_=src, func=AF.Copy),
                             reads=[SbK[c % 2]], writes=[("sbf", par, c)])
                        P.op(DVE, lambda e, src=src, dst=dst, U=U, dcol=dcol: e.scalar_tensor_tensor(
                            out=dst, in0=src, scalar=dcol, in1=U, op0=ALU.mult, op1=ALU.add),
                            reads=[SbK[c % 2], ("egl", a)] + pk(ub[c % 2]), writes=[SbK[(c + 1) % 2]])
                    if c1_ == NCH and lastp:
                        P.dma(SP, lambda e: e.dma_start(out=sp_out[hd], in_=Sb[0]), reads=[SbK[0]],
                              sem_key=("spo", hd % 4), is_output=True)

                def sample_states(a):
                    hd = hg * 8 + a
                    par = a % 2
                    ub = ctx[("ub", a)]
                    ip = a % 2
                    Sin = sstate[:, 4 * ip:4 * ip + 4, :]
                    Sout = sstate[:, 8 + 4 * ip:8 + 4 * ip + 4, :]
                    SKi = [("S", 4 * ip + j) for j in range(4)]
                    SKo = [("S", 8 + 4 * ip + j) for j in range(4)]
                    P.dma(SP, lambda e: e.dma_start(out=Sin, in_=st_in[:, hd].rearrange("j k v -> k j v")),
                          writes=SKi, sem_key=("sin", ip))
                    P.op(ACT, lambda e: e.activation(out=sbf[:, par, 0:4, :], in_=Sin, func=AF.Copy),
                         reads=SKi, writes=[("sbf", par, c) for c in range(NCH)])
                    for c in range(NCH):
                        U = ps[ub[c % 2]][:, 128 * (c // 2):128 * (c // 2) + 128]
                        dcol = egl[:, a, c:c + 1]
                        P.op(DVE, lambda e, c=c, U=U, dcol=dcol: e.scalar_tensor_tensor(
                            out=Sout[:, c, :], in0=Sin[:, c, :], scalar=dcol, in1=U, op0=ALU.mult, op1=ALU.add),
                            reads=SKi + pk(ub[c % 2]) + [("egl", a)], writes=[SKo[c]])
                    P.dma(SP, lambda e: e.dma_start(out=ss_out[:, hd].rearrange("j k v -> k j v"), in_=Sout),
                          reads=SKo, sem_key=("sout", ip), is_output=True)

                def side(prev, part):
                    if prev is None:
                        return
                    if is_s:
                        if part == 0:
                            sample_states(prev)
                        return
                    bounds = [0, NCH // 3 + 1, 2 * NCH // 3 + 1, NCH]
                    chain_steps(prev, bounds[part], bounds[part + 1])

                def mask_A(a):
                    par = a % 2
                    ba = ctx[("ba", a)]
                    A_sb = sq[par][:, 0:T].rearrange("p (b t) -> p b t", t=128)
                    P.op(DVE, lambda e: e.tensor_tensor(
                        out=A_sb, in0=ps[ba][:, 0:T].rearrange("p (b t) -> p b t", t=128),
                        in1=bdmask[:].unsqueeze(1).to_broadcast([128, NB, 128]), op=ALU.mult),
                        reads=pk(ba) + ["bdmask"], writes=[("sq", par)])

                def B1pe(a):
                    ub = [bank("g"), bank("g")]
                    ctx[("ub", a)] = ub
                    for c in range(NCH):
                        b, hf = c // 2, c % 2
                        P.op(PE, lambda e, b=b, hf=hf, ubk=ub[hf]: e.matmul(
                            ps[ubk][:, 128 * b:128 * b + 128], lhsT=k3T[64 * hf:64 * hf + 64, a, b, :],
                            rhs=v_sb[64 * hf:64 * hf + 64, b, 128 * a:128 * a + 128], start=True, stop=True),
                            reads=k3k(a) + VK, writes=pk(ub[hf]))
                    ba = bank("aux")
                    ctx[("ba", a)] = ba
                    for b in range(NB):
                        P.op(PE, lambda e, b=b: e.matmul(
                            ps[ba][:, 128 * b:128 * b + 128], lhsT=k2[:, a, 128 * b:128 * b + 128],
                            rhs=q1[:, a, 128 * b:128 * b + 128], start=True, stop=True),
                            reads=k2k(a) + q1k(a), writes=pk(ba))

                def S2to6(a, prev, pre_t=None):
                    bq, bf = ctx[a]
                    hd = hg * 8 + a
                    (t0_, t1_, t2_, t3_), TK = tms(a)
                    qps, fps = ps[bq][:, 0:T], ps[bf][:, 0:T]
                    ln1m = lbt[:, 7, hd:hd + 1]
                    P.op(DVE, lambda e: e.tensor_tensor(out=t2_, in0=t2_, in1=t1_, op=ALU.subtract),
                         reads=[TK[1], TK[2]], writes=[TK[2]])
                    P.op(DVE, lambda e: e.tensor_tensor_scan(out=t3_, data0=scanmask[:, 0:T], data1=t2_, initial=0.0,
                                                             op0=ALU.mult, op1=ALU.add),
                         reads=[TK[2], "scanmask"], writes=[TK[3]])
                    P.op(DVE, lambda e: e.tensor_tensor(out=t0_, in0=t3_, in1=t0_, op=ALU.subtract),
                         reads=[TK[0], TK[3]], writes=[TK[0]])
                    P.op(ACT, lambda e: e.activation(out=t0_, in_=t0_, func=AF.Exp), reads=[TK[0]], writes=[TK[0]])
                    side(prev, 0)
                    P.op(DVE, lambda e: e.tensor_tensor(out=q1[:, a, 0:T], in0=qps, in1=t0_, op=ALU.mult),
                         reads=pk(bq) + [TK[0]], writes=q1k(a))
                    P.op(DVE, lambda e: e.tensor_tensor(out=t1_, in0=fps, in1=t1_, op=ALU.add),
                         reads=pk(bf) + [TK[1]], writes=[TK[1]])
                    P.op(DVE, lambda e: e.tensor_tensor(out=t1_, in0=t1_, in1=t3_, op=ALU.add),
                         reads=[TK[1], TK[3]], writes=[TK[1]])
                    P.op(ACT, lambda e: e.activation(out=t2_, in_=t1_, func=AF.Exp, scale=-1.0, bias=ln1m),
                         reads=[TK[1], "lbt"], writes=[TK[2]])
                    P.op(ACT, lambda e: e.activation(
                        out=egl[:, a, 0:NCH], in_=t3_.rearrange("p (c t) -> p c t", t=64)[:, :, 63], func=AF.Exp),
                        reads=[TK[3]], writes=[("egl", a)])
                    P.op(ACT, lambda e: e.activation(out=k2[:, a, 0:T], in_=t1_, func=AF.Exp, scale=-1.0, bias=ln1m),
                         reads=[TK[1], "lbt"], writes=k2k(a))
                    side(prev, 1)
                    P.op(DVE, lambda e: e.tensor_tensor(
                        out=t2_.rearrange("p (c t) -> p c t", t=64), in0=t2_.rearrange("p (c t) -> p c t", t=64),
                        in1=egl[:, a, 0:NCH].unsqueeze(2).to_broadcast([128, NCH, 64]), op=ALU.mult),
                        reads=[TK[2], ("egl", a)], writes=[TK[2]])
                    side(prev, 2)
                    if prev is not None:
                        mask_A(prev)
                    if pre_t is not None:
                        pre_t()
                    bt = bank("aux")
                    for b in range(NB):
                        P.op(PE, lambda e, b=b: e.transpose(out=ps[bt][:, 128 * b:128 * b + 128],
                                                            in_=t2_[:, 128 * b:128 * b + 128], identity=ident[:]),
                             reads=[TK[2], "ident"], writes=pk(bt))
                    evac(k3T[:, a, 0:NB, :], ps[bt][:, 0:T].rearrange("p (b d) -> p b d", d=128), pk(bt), k3k(a))

                def B2(a):
                    hd = hg * 8 + a
                    par = a % 2
                    A_sb = sq[par][:, 0:T].rearrange("p (b t) -> p b t", t=128)
                    bo = bank("aux")
                    for b in range(NB):
                        P.op(PE, lambda e, b=b: e.matmul(
                            ps[bo][:, 128 * b:128 * b + 128], lhsT=v_sb[:, b, 128 * a:128 * a + 128],
                            rhs=A_sb[:, b, :], start=True, stop=False), reads=VK + [("sq", par)], writes=pk(bo))
                        for hf in range(2):
                            c = 2 * b + hf
                            P.op(PE, lambda e, b=b, hf=hf, c=c: e.matmul(
                                ps[bo][:, 128 * b + 64 * hf:128 * b + 64 * hf + 64], lhsT=sbf[:, par, c, :],
                                rhs=q1[:, a, 128 * b + 64 * hf:128 * b + 64 * hf + 64], start=False, stop=(hf == 1)),
                                reads=[("sbf", par, c)] + q1k(a), writes=pk(bo))
                    evac(o_sb[:, hd, 0:T], ps[bo][:, 0:T], pk(bo), okeys(hd))

                V()
                if hstop == 1:
                    return
                Pj(0)
                S1(0)
                for a in range(8):
                    if a >= 1:
                        B1pe(a - 1)
                    if a + 1 < 8:
                        Pj(a + 1)
                    S2to6(a, a - 1 if a >= 1 else None, (lambda a=a: B2(a - 2)) if a >= 2 else None)
                    if a + 1 < 8:
                        S1(a + 1)
                B1pe(7)
                for part in range(3):
                    side(7, part)
                mask_A(7)
                B2(6)
                B2(7)
            if hstop == 4:
                return
            rms_stats(lambda kc: o_sb[:, kc, 0:T], okeys, T, 1.0 / D)
            y = Q[:, 0:KC * TMAX].rearrange("p (k t) -> p k t", t=TMAX)

            def yk(kc):
                return rk("Q", kc * TMAX, kc * TMAX + TMAX)

            for s in range(4):
                (wg_,), kg_ = wload([(w_in_v[:, :, 6144 + 512 * s:6144 + 512 * s + 512], KC, 512)])
                for hh in range(4):
                    hd = 4 * s + hh
                    bg_ = proj_fm(wg_, kg_, 128 * hh, lambda k: xn[:, k, 0:T], xk, KC, T)
                    ta, tb = tmp[2 * (hd % 2)], tmp[2 * (hd % 2) + 1]
                    ka, kb_ = ("tmp", 2 * (hd % 2)), ("tmp", 2 * (hd % 2) + 1)
                    P.op(ACT, lambda e, ta=ta, bg_=bg_: e.activation(out=ta[:, 0:T], in_=ps[bg_][:, 0:T], func=AF.Silu),
                         reads=pk(bg_), writes=[ka])
                    P.op(DVE, lambda e, tb=tb, hd=hd: e.tensor_tensor(out=tb[:, 0:T], in0=o_sb[:, hd, 0:T],
                                                                      in1=rstd[:, 0:T], op=ALU.mult),
                         reads=okeys(hd) + ["rstd"], writes=[kb_])
                    P.op(DVE, lambda e, ta=ta, tb=tb, hd=hd: e.scalar_tensor_tensor(
                        out=y[:, hd, 0:T], in0=tb[:, 0:T], scalar=gcols[:, 6, hd:hd + 1], in1=ta[:, 0:T],
                        op0=ALU.mult, op1=ALU.mult), reads=[ka, kb_, "gcols"], writes=yk(hd))
            for s in range(4):
                (wo_,), ko_ = wload([(a_wo_v[:, :, 512 * s:512 * s + 512], KC, 512)])
                for o4 in range(4):
                    oc = 4 * s + o4
                    bk = proj_fm(wo_, ko_, 128 * o4, lambda k: y[:, k, 0:T], yk, KC, T)
                    resid_add(oc, bk, T)

        def swa(tile, T, first_prompt_tile):
            NB = T // 128
            is_s = tile["kind"] == "s"
            rmsnorm_h(1, T)
            qn = R[:, 0:KC * TMAX].rearrange("p (k t) -> p k t", t=TMAX)
            kT = Q[:, 0:2560].rearrange("p (g t) -> p g t", t=640)
            vS = Q[:, 2560:5120].rearrange("p (b g f) -> p b g f", g=4, f=128)
            pTt = Q[:, 5120:6144].rearrange("p (k t) -> p k t", t=512)
            kfd = [Qf[:, 3072 + 512 * i:3072 + 512 * i + 512].rearrange("p (g r d) -> p g r d", r=2, d=64)
                   for i in range(2)]
            vf = [Qf[:, 4096 + 256 * i:4096 + 256 * i + 256] for i in range(2)]

            def qnk(kc):
                return rk("R", kc * TMAX, kc * TMAX + TMAX)

            def kTk(blk):
                return rk("Q", 0, 2560)

            def vSk(blk):
                return rk("Q", 2560, 5120)

            PTK = [("Q", 10), ("Q", 11)]
            KFD = [rk("Q", 6144 + 1024 * i, 6144 + 1024 * i + 1024) for i in range(2)]
            VF = [rk("Q", 8192 + 512 * i, 8192 + 512 * i + 512) for i in range(2)]

            if not is_s and not first_prompt_tile:
                P.op(DVE, lambda e: e.tensor_copy(out=kT[:, :, 0:128], in_=kcar[:]), reads=["kcar"], writes=kTk(0))
                P.op(DVE, lambda e: e.tensor_copy(out=vS[:, 0], in_=vcar[:]), reads=["vcar"], writes=vSk(0))
            for s in range(4):
                (wq_,), kq_ = wload([(wqkv_v[:, :, 512 * s:512 * s + 512], KC, 512)])
                for c4 in range(4):
                    pc = 4 * s + c4
                    bq = proj_fm(wq_, kq_, 128 * c4, lambda k: xn[:, k, 0:T], xk, KC, T)
                    s_ = sq[pc % 2]
                    P.op(ACT, lambda e, s_=s_, bq=bq: e.activation(out=s_[:, 0:T], in_=ps[bq][:, 0:T], func=AF.Square),
                         reads=pk(bq), writes=[("sq", pc % 2)])
                    bs = bank("aux")
                    P.op(PE, lambda e, s_=s_, bs=bs: e.matmul(ps[bs][:, 0:T], lhsT=bd_bf[:], rhs=s_[:, 0:T],
                                                             start=True, stop=True),
                         reads=[("sq", pc % 2), "bd"], writes=pk(bs))
                    t_ = tmp[pc % 2]
                    P.op(ACT, lambda e, t_=t_, bs=bs: e.activation(out=t_[:, 0:T], in_=ps[bs][:, 0:T], func=AF.Ln,
                                                                   scale=1.0 / 64, bias=EPS),
                         reads=pk(bs), writes=[("tmp", pc % 2)])
                    P.op(ACT, lambda e, t_=t_: e.activation(out=t_[:, 0:T], in_=t_[:, 0:T], func=AF.Exp, scale=-0.5),
                         reads=[("tmp", pc % 2)], writes=[("tmp", pc % 2)])
                    P.op(DVE, lambda e, t_=t_, bq=bq, pc=pc: e.scalar_tensor_tensor(
                        out=qn[:, pc, 0:T], in0=ps[bq][:, 0:T], scalar=qg[:, 0:1], in1=t_[:, 0:T],
                        op0=ALU.mult, op1=ALU.mult), reads=pk(bq) + [("tmp", pc % 2), "qg"], writes=qnk(pc))
            sstop = (dbg or {}).get("hstop", 99)
            if sstop == 10:
                return
            (wkv_,), kkv_ = wload([(wqkv_v[:, :, 2048:2560], KC, 512)])
            kv_deferred = [None]
            for b in range(NB):
                i2 = b % 2
                bk = bank("mm")
                for k in range(KC):
                    P.op(PE, lambda e, k=k, b=b, bk=bk: e.matmul(ps[bk][:, :], lhsT=xn[:, k, 128 * b:128 * b + 128],
                                                               rhs=wkv_[:, k, :], start=(k == 0), stop=(k == KC - 1)),
                         reads=kkv_ + xk(k), writes=pk(bk))
                kps = ps[bk][:, 0:256]
                vps = ps[bk][:, 256:512]
                if kv_deferred[0] is not None:
                    kv_deferred[0]()
                    kv_deferred[0] = None
                P.op(ACT, lambda e, i2=i2, vps=vps: e.activation(out=vf[i2], in_=vps, func=AF.Copy),
                     reads=pk(bk), writes=VF[i2])
                t_ = tmp[2 + i2]
                tk_ = ("tmp", 2 + i2)
                P.op(ACT, lambda e, t_=t_, kps=kps: e.activation(out=t_[:, 0:256], in_=kps, func=AF.Square),
                     reads=pk(bk), writes=[tk_])
                P.op(DVE, lambda e, b=b, vps=vps: e.tensor_copy(
                    out=vS[:, 1 + b].rearrange("p g (r d) -> p g r d", r=2),
                    in_=vps.rearrange("p (g d) -> p g d", d=64).unsqueeze(2).to_broadcast([128, 4, 2, 64])),
                    reads=pk(bk) + VF[i2] + [tk_], writes=vSk(1 + b))
                if sstop == 13:
                    continue
                P.op(DVE, lambda e, t_=t_, i2=i2: e.tensor_reduce(
                    out=ksm[:, i2, :], in_=t_[:, 0:256].rearrange("p (g d) -> p g d", d=64), axis=AX.X, op=ALU.add),
                    reads=[tk_], writes=[("ksm", i2)])
                P.op(ACT, lambda e, i2=i2: e.activation(out=ksm[:, i2, :], in_=ksm[:, i2, :], func=AF.Ln,
                                                        scale=1.0 / 64, bias=EPS),
                     reads=[("ksm", i2)], writes=[("ksm", i2)])
                P.op(ACT, lambda e, i2=i2: e.activation(out=ksm[:, i2, :], in_=ksm[:, i2, :], func=AF.Exp, scale=-0.5),
                     reads=[("ksm", i2)], writes=[("ksm", i2)])
                P.op(DVE, lambda e, t_=t_, kps=kps, i2=i2: e.tensor_tensor(
                    out=t_[:, 0:256].rearrange("p (g d) -> p g d", d=64),
                    in0=kps.rearrange("p (g d) -> p g d", d=64),
                    in1=ksm[:, i2, :].unsqueeze(2).to_broadcast([128, 4, 64]), op=ALU.mult),
                    reads=pk(bk) + [("ksm", i2)], writes=[tk_])
                for r in range(2):
                    P.op(DVE, lambda e, t_=t_, i2=i2, r=r: e.tensor_tensor(
                        out=kfd[i2][:, :, r, :], in0=t_[:, 0:256].rearrange("p (g d) -> p g d", d=64),
                        in1=kg_bc[:].unsqueeze(1).to_broadcast([128, 4, 64]), op=ALU.mult),
                        reads=[tk_, "kg"], writes=KFD[i2])
                if sstop == 14:
                    continue
                if is_s:
                    P.dma(SP, lambda e, b=b, i2=i2: e.dma_start(
                        out=cks[128 * b:128 * b + 128, :].rearrange("t (g d) -> t g d", d=64), in_=kfd[i2][:, :, 0, :]),
                        reads=KFD[i2], sem_key=("cko", i2), is_output=True)
                    P.dma(SP, lambda e, b=b, i2=i2: e.dma_start(out=cvs[128 * b:128 * b + 128, :], in_=vf[i2]),
                          reads=VF[i2], sem_key=("cvo", i2), is_output=True)
                elif tile["t0"] + T == SEQ and b == NB - 1:
                    P.dma(SP, lambda e, i2=i2: e.dma_start(
                        out=ckp[:, :].rearrange("t (g d) -> t g d", d=64), in_=kfd[i2][:, :, 0, :]),
                        reads=KFD[i2], sem_key=("cko", i2), is_output=True)
                    P.dma(SP, lambda e, i2=i2: e.dma_start(out=cvp[:, :], in_=vf[i2]),
                          reads=VF[i2], sem_key=("cvo", i2), is_output=True)
                if sstop == 15:
                    continue
                def ktrans(b=b, i2=i2):
                    bt = bank("aux")
                    for g in range(4):
                        P.op(PE, lambda e, g=g, bt=bt, i2=i2: e.transpose(
                            out=ps[bt][:, 128 * g:128 * g + 128], in_=kfd[i2][:, g].rearrange("p r d -> p (r d)"),
                            identity=ident[:]), reads=KFD[i2] + ["ident"], writes=pk(bt))
                    evac(kT[:, :, 128 + 128 * b:128 + 128 * b + 128],
                         ps[bt][:, :].rearrange("p (g t) -> p g t", t=128), pk(bt), kTk(1 + b))

                kv_deferred[0] = ktrans
            if kv_deferred[0] is not None:
                kv_deferred[0]()
            if not is_s:
                P.op(DVE, lambda e: e.tensor_copy(out=kcar[:], in_=kT[:, :, TMAX:TMAX + 128]), reads=kTk(4),
                     writes=["kcar"])
                P.op(DVE, lambda e: e.tensor_copy(out=vcar[:], in_=vS[:, 4]), reads=vSk(4), writes=["vcar"])
            if is_s:
                cstk = Rf[:, 4096:4096 + 1024].rearrange("p (j f) -> p j f", f=256)
                cstv = Rf[:, 5120:5120 + 1024].rearrange("p (j f) -> p j f", f=256)
                kTc = R[:, 12288:14336].rearrange("p (j g t) -> p j g t", g=4, t=128)
                vc = R[:, 14336:16384].rearrange("p (j g f) -> p j g f", g=4, f=128)
                CK, CV = rk("R", 8192, 10240), rk("R", 10240, 12288)
                KTC, VC = rk("R", 12288, 14336), rk("R", 14336, 16384)
                P.dma(SP, lambda e: e.dma_start(out=cstk, in_=ck_in.rearrange("j t f -> t j f")), writes=CK,
                      sem_key="ckin")
                P.dma(SP, lambda e: e.dma_start(out=cstv, in_=cv_in.rearrange("j t f -> t j f")), writes=CV,
                      sem_key="cvin")
                for j in range(NSEQ_S):
                    i2 = j % 2
                    P.op(DVE, lambda e, j=j, i2=i2: e.tensor_copy(
                        out=kfd[i2][:], in_=cstk[:, j, :].rearrange("p (g d) -> p g d", d=64).unsqueeze(2)
                        .to_broadcast([128, 4, 2, 64])), reads=CK, writes=KFD[i2])
                    bt = bank("aux")
                    for g in range(4):
                        P.op(PE, lambda e, g=g, bt=bt, i2=i2: e.transpose(
                            out=ps[bt][:, 128 * g:128 * g + 128], in_=kfd[i2][:, g].rearrange("p r d -> p (r d)"),
                            identity=ident[:]), reads=KFD[i2] + ["ident"], writes=pk(bt))
                    evac(kTc[:, j], ps[bt][:, :].rearrange("p (g t) -> p g t", t=128), pk(bt), KTC)
                    P.op(DVE, lambda e, j=j: e.tensor_copy(
                        out=vc[:, j].rearrange("p g (r d) -> p g r d", r=2),
                        in_=cstv[:, j, :].rearrange("p (g d) -> p g d", d=64).unsqueeze(2)
                        .to_broadcast([128, 4, 2, 64])), reads=CV, writes=VC)
            if sstop in (11, 13, 14, 15):
                return
            attn_o = xn
            groups = []
            if is_s:
                for j in range(NSEQ_S):
                    b = j // 2
                    c0 = 64 * j
                    groups.append(dict(c0=c0, nt=64, kbs=[
                        dict(kT=lambda g, j=j: kTc[:, j, g, :], v=lambda g, j=j: vc[:, j, g, :],
                             Dm=D0[:, 0:64], keys=KTC + VC),
                        dict(kT=lambda g, b=b: kT[:, g, 128 + 128 * b:128 + 128 * b + 128],
                             v=lambda g, b=b: vS[:, 1 + b, g, :], Dm=D1s[:, 64 * (j % 2):64 * (j % 2) + 64],
                             keys=kTk(1 + b) + vSk(1 + b))]))
            else:
                for b in range(NB):
                    kbs = []
                    if not (first_prompt_tile and b == 0):
                        kbs.append(dict(kT=lambda g, b=b: kT[:, g, 128 * b:128 * b + 128],
                                        v=lambda g, b=b: vS[:, b, g, :], Dm=D0[:, :], keys=kTk(b) + vSk(b)))
                    kbs.append(dict(kT=lambda g, b=b: kT[:, g, 128 + 128 * b:128 + 128 * b + 128],
                                    v=lambda g, b=b: vS[:, 1 + b, g, :], Dm=D1[:, :], keys=kTk(1 + b) + vSk(1 + b)))
                    groups.append(dict(c0=128 * b, nt=128, kbs=kbs))
            pT2 = [pTt, Q[:, 9216:10240].rearrange("p (k t) -> p k t", t=512)]
            PT2K = [PTK, [("Q", 18), ("Q", 19)]]
            den_buf = [tmp[4], tmp[5]]
            den_key = [("tmp", 4), ("tmp", 5)]
            items = [(gr, g, par) for gr in groups for g in range(4) for par in range(2)]

            def stage1(it, part):
                gr, g, par = it
                c0, nt, kbs = gr["c0"], gr["nt"], gr["kbs"]
                N = 4 * nt
                lo, hi = 64 * par, 64 * par + 64
                rhs_q = qn[lo:hi, 4 * g:4 * g + 4, c0:c0 + nt]
                qkeys = [k for pc in range(4 * g, 4 * g + 4) for k in qnk(pc)]
                for ki, kb in enumerate(kbs):
                    t_ = tmp[2 * par + ki]
                    tk = ("tmp", 2 * par + ki)
                    if part == 1:
                        P.op(ACT, lambda e, t_=t_, ki=ki, N=N, par=par: e.activation(
                            out=pT2[par][:, ki, 0:N], in_=t_[:, 0:N], func=AF.Exp, scale=0.125),
                            reads=[tk], writes=[PT2K[par][ki]])
                        continue
                    bs = bank(f"g{par}")
                    P.op(PE, lambda e, kb=kb, bs=bs, rhs_q=rhs_q, g=g, lo=lo, hi=hi, N=N, nt=nt: e.matmul(
                        ps[bs][:, 0:N].rearrange("p (a t) -> p a t", t=nt), lhsT=kb["kT"](g)[lo:hi, :],
                        rhs=rhs_q, start=True, stop=True), reads=kb["keys"] + qkeys, writes=pk(bs))
                    t_ = tmp[2 * par + ki]
                    tk = ("tmp", 2 * par + ki)
                    for pr in range(4):
                        head = 8 * g + 2 * pr + par
                        sl = -8.0 * alibi_slope(head)
                        P.op(DVE, lambda e, t_=t_, pr=pr, kb=kb, bs=bs, sl=sl, nt=nt: e.scalar_tensor_tensor(
                            out=t_[:, pr * nt:(pr + 1) * nt], in0=kb["Dm"], scalar=sl,
                            in1=ps[bs][:, pr * nt:(pr + 1) * nt], op0=ALU.mult, op1=ALU.add),
                            reads=pk(bs) + ["D0", "D1", "D1s"], writes=[tk])

            s2ctx = {}

            def stage2(it, part):
                gr, g, par = it
                c0, nt, kbs = gr["c0"], gr["nt"], gr["kbs"]
                N = 4 * nt
                nkb = len(kbs)
                lo, hi = 64 * par, 64 * par + 64
                t2_ = den_buf[par]
                dk_ = den_key[par]
                if part == 1:
                    bo_ = s2ctx[par]
                    P.op(DVE, lambda e, t2_=t2_, bo_=bo_, g=g, lo=lo, hi=hi, N=N, nt=nt, c0=c0: e.tensor_tensor(
                        out=attn_o[lo:hi, 4 * g:4 * g + 4, c0:c0 + nt],
                        in0=ps[bo_][lo:hi, 0:N].rearrange("p (a t) -> p a t", t=nt),
                        in1=t2_[lo:hi, 0:N].rearrange("p (a t) -> p a t", t=nt), op=ALU.mult),
                        reads=pk(bo_) + [dk_], writes=[("xn", pc) for pc in range(4 * g, 4 * g + 4)])
                    return
                bd_ = bank("aux")
                bo_ = bank("mm01")
                s2ctx[par] = bo_
                for ki, kb in enumerate(kbs):
                    P.op(PE, lambda e, ki=ki, bd_=bd_, N=N, nkb=nkb, par=par: e.matmul(
                        ps[bd_][:, 0:N], lhsT=ones_bf[:], rhs=pT2[par][:, ki, 0:N], start=(ki == 0),
                        stop=(ki == nkb - 1)), reads=[PT2K[par][ki], "ones"], writes=pk(bd_))
                for ki, kb in enumerate(kbs):
                    P.op(PE, lambda e, ki=ki, kb=kb, bo_=bo_, g=g, N=N, nkb=nkb, par=par: e.matmul(
                        ps[bo_][:, 0:N], lhsT=kb["v"](g), rhs=pT2[par][:, ki, 0:N], start=(ki == 0),
                        stop=(ki == nkb - 1)), reads=[PT2K[par][ki]] + kb["keys"], writes=pk(bo_))
                P.op(DVE, lambda e, t2_=t2_, bd_=bd_, g=g, par=par, N=N, nt=nt: e.tensor_tensor(
                    out=t2_[:, 0:N].rearrange("p (a t) -> p a t", t=nt),
                    in0=ps[bd_][:, 0:N].rearrange("p (a t) -> p a t", t=nt),
                    in1=esink[:, 8 * g + par:8 * g + 8:2].unsqueeze(2).to_broadcast([128, 4, nt]),
                    op=ALU.add), reads=pk(bd_) + ["esink"], writes=[dk_])
                P.op(ACT, lambda e, t2_=t2_, N=N: e.activation(out=t2_[:, 0:N], in_=t2_[:, 0:N], func=AF.Ln),
                     reads=[dk_], writes=[dk_])
                P.op(ACT, lambda e, t2_=t2_, N=N: e.activation(out=t2_[:, 0:N], in_=t2_[:, 0:N], func=AF.Exp,
                                                               scale=-1.0), reads=[dk_], writes=[dk_])

            NI = len(items)
            assert all(items[i][2] == i % 2 for i in range(NI))
            stage1(items[0], 0)
            if NI > 1:
                stage1(items[1], 0)
            stage1(items[0], 1)
            for n in range(NI):
                stage2(items[n], 0)
                if n + 2 < NI:
                    stage1(items[n + 2], 0)
                stage2(items[n], 1)
                if n + 1 < NI:
                    stage1(items[n + 1], 1)
            if sstop == 12:
                return
            for s in range(4):
                (wo_,), ko_ = wload([(b_wo_v[:, :, 512 * s:512 * s + 512], KC, 512)])
                for o4 in range(4):
                    oc = 4 * s + o4
                    bk = proj_fm(wo_, ko_, 128 * o4, lambda k: attn_o[:, k, 0:T], xk, KC, T)
                    resid_add(oc, bk, T)

        stop_after = (dbg or {}).get("stop_after")
        first_p = True
        for ti, tile in enumerate(tiles):
            T = 512 if tile["kind"] == "p" else NSEQ_S * LS
            wtile[0], wtile[1] = 0, ti
            if ti == 0 or stop_after is not None:
                issue_x(tile, T)
            unpack_x(tile, T)
            phases = [lambda: hgrn(tile, T), lambda: ffn(0, T), lambda: ple(0, tile, T),
                      lambda: swa(tile, T, first_p), lambda: ffn(1, T), lambda: ple(1, tile, T)]
            for i, ph in enumerate(phases):
                if stop_after is not None and i > stop_after:
                    break
                if i == 5 and stop_after is None and ti + 1 < len(tiles):
                    nt_ = tiles[ti + 1]
                    issue_x(nt_, 512 if nt_["kind"] == "p" else NSEQ_S * LS)
                ph()
            if tile["kind"] == "p":
                first_p = False
            store_y(tile, T)
        with nc.allow_non_contiguous_dma(reason="tiny per-feature vectors / strided cache rows"):
            P.emit()
    return nc


TILES_FULL = [dict(kind="p", t0=512 * i) for i in range(4)] + [dict(kind="s")]
N_CORES = 8


def make_in_maps(inp, cores):
    f = lambda a: np.ascontiguousarray(np.asarray(a, dtype=np.float32))
    shared = dict(
        nmix=f(inp["norm_mix"]), nffn=f(inp["norm_ffn"]), nple=f(inp["norm_ple"]),
        w_in=f(inp["a_w_in"][0]), lbl=f(inp["a_lb_logits"]), gnorm=f(inp["a_g_norm"]),
        a_wo=f(inp["a_w_o"][0]), wqkv=f(inp["b_w_qkv"][0]), qnorm=f(inp["b_q_norm"]),
        knorm=f(inp["b_k_norm"]), sinks=f(inp["b_sinks"]), b_wo=f(inp["b_w_o"][0]),
        wgu=f(inp["f_w_gu"]), wdown=f(inp["f_w_down"]), wpp=f(inp["ple_w_proj"]), wpg=f(inp["ple_w_gate"]))
    maps = []
    for c in cores:
        s0, s1 = NSEQ_S * c, NSEQ_S * c + NSEQ_S
        m = dict(shared)
        m.update(
            xp=f(inp["x_prompt"][c]),
            xs=f(inp["x_sample"][s0:s1]).reshape(NSEQ_S * LS, D),
            st=f(inp["state_hgrn"][0, s0:s1]),
            ck=f(inp["cache_k"][0, s0:s1]).reshape(NSEQ_S, 128, 256),
            cv=f(inp["cache_v"][0, s0:s1]).reshape(NSEQ_S, 128, 256),
            pp=f(inp["p_prompt"][:, c]),
            psm=f(inp["p_sample"][:, s0:s1]).reshape(2, NSEQ_S * LS, 256))
        maps.append(m)
    return maps


def kernel(**inp):
    nc = build(TILES_FULL)
    cores = list(range(N_CORES))
    res = run_bass_kernel_spmd(nc, make_in_maps(inp, cores), core_ids=cores)
    r = res.results
    y_prompt = np.stack([r[c]["yp"] for c in cores]).astype(np.float32)
    y_sample = np.concatenate([r[c]["ys"].reshape(NSEQ_S, LS, D) for c in cores]).astype(np.float32)
    st_p = np.stack([r[c]["sp"] for c in cores])[None].astype(np.float32)
    st_s = np.concatenate([r[c]["ss"] for c in cores])[None].astype(np.float32)
    ck_p = np.stack([r[c]["ckp"].reshape(128, 4, 64) for c in cores])[None].astype(np.float32)
    cv_p = np.stack([r[c]["cvp"].reshape(128, 4, 64) for c in cores])[None].astype(np.float32)
    ck_s = np.concatenate([r[c]["cks"].reshape(NSEQ_S, LS, 4, 64) for c in cores])[None].astype(np.float32)
    cv_s = np.concatenate([r[c]["cvs"].reshape(NSEQ_S, LS, 4, 64) for c in cores])[None].astype(np.float32)
    return (y_prompt, y_sample, st_p, st_s, ck_p, cv_p, ck_s, cv_s)
```
